# Optimizing a Trainium2 kernel written in Bass

```python
import math
import jax, jax.numpy as jnp
from jax import lax
import numpy as np

D_MODEL = 2048
BATCH = 4
SEQ = 4096
DEPTH = 4

D_MIX = D_MODEL
ATTN_WIDTH = D_MIX // 2
POOL_WIDTH = D_MIX - ATTN_WIDTH
HEAD_DIM = 64
N_HEADS = ATTN_WIDTH // HEAD_DIM
N_KV_HEADS = N_HEADS // 4
GQA_GROUP = N_HEADS // N_KV_HEADS
WINDOW = 128
BLOCK = 128
POOL_SIZES = (2, 4, 8, 16)
N_POOL_GROUPS = len(POOL_SIZES)
POOL_GROUP_DIM = POOL_WIDTH // N_POOL_GROUPS
D_FF = int(round(8 * D_MODEL / 3 / 128)) * 128
CONV_WIDTH = 3
N_BUCKETS = 32
MAX_DISTANCE = 128
DEEPNORM_ALPHA = (2 * DEPTH) ** 0.25
DEEPNORM_BETA = (8 * DEPTH) ** -0.25
LN_EPS = 1e-5
MASK_VALUE = -1e30
Q_COLS = N_HEADS * HEAD_DIM
KV_COLS = N_KV_HEADS * HEAD_DIM
IN_COLS = Q_COLS + 2 * KV_COLS + POOL_WIDTH

kernel_name = "hymba_style_window_gqa_multiscale_pool_convglu_deepnorm"


def layer_norm(x, g, b):
    xf = x.astype(jnp.float32)
    mu = jnp.mean(xf, axis=-1, keepdims=True)
    var = jnp.mean(jnp.square(xf - mu), axis=-1, keepdims=True)
    return ((xf - mu) * lax.rsqrt(var + LN_EPS) * g.astype(jnp.float32) + b.astype(jnp.float32)).astype(x.dtype)


def t5_bucket(rel):
    half = N_BUCKETS // 2
    max_exact = half // 2
    base = jnp.where(rel > 0, half, 0)
    n = jnp.abs(rel)
    nf = jnp.maximum(n, 1).astype(jnp.float32)
    large = max_exact + (jnp.log(nf / max_exact) / math.log(MAX_DISTANCE / max_exact)
                         * (half - max_exact)).astype(jnp.int32)
    large = jnp.minimum(large, half - 1)
    return base + jnp.where(n < max_exact, n, large)


def banded_bias_and_mask(rel_bias, seq):
    n_blocks = seq // BLOCK
    q_off = jnp.arange(BLOCK)[:, None]
    k_off = jnp.arange(3 * BLOCK)[None, :] - BLOCK
    rel = k_off - q_off
    bias = jnp.transpose(rel_bias[t5_bucket(rel)], (2, 0, 1))
    band = jnp.abs(rel) <= WINDOW
    key_pos = jnp.arange(n_blocks)[:, None] * BLOCK + k_off
    valid = (key_pos >= 0) & (key_pos < seq)
    mask = band[None] & valid[:, None, :]
    return bias, mask


def windowed_gqa(q, k, v, sink, pos_bias, mask):
    b, s = q.shape[0], q.shape[1]
    nb = s // BLOCK
    qb = q.reshape(b, nb, BLOCK, N_KV_HEADS, GQA_GROUP, HEAD_DIM)

    def band(t):
        tp = jnp.pad(t, ((0, 0), (BLOCK, BLOCK), (0, 0), (0, 0)))
        tp = tp.reshape(b, nb + 2, BLOCK, N_KV_HEADS, HEAD_DIM)
        return jnp.concatenate([tp[:, :-2], tp[:, 1:-1], tp[:, 2:]], axis=2)

    kb, vb = band(k), band(v)
    scores = jnp.einsum('bnqkgd,bnskd->bnkgqs', qb, kb).astype(jnp.float32) * (HEAD_DIM ** -0.5)
    scores = scores + pos_bias.reshape(N_KV_HEADS, GQA_GROUP, BLOCK, 3 * BLOCK).astype(jnp.float32)
    scores = jnp.where(mask[None, :, None, None], scores, MASK_VALUE)
    sink_l = sink.astype(jnp.float32).reshape(N_KV_HEADS, GQA_GROUP)[None, None, :, :, None, None]
    m = jnp.maximum(jnp.max(scores, axis=-1, keepdims=True), sink_l)
    p = jnp.exp(scores - m)
    denom = jnp.sum(p, axis=-1, keepdims=True) + jnp.exp(sink_l - m)
    probs = (p / denom).astype(v.dtype)
    out = jnp.einsum('bnkgqs,bnskd->bnqkgd', probs, vb)
    return out.reshape(b, s, N_HEADS * HEAD_DIM)


def multiscale_pool(p, w_pool, pool_scale):
    b, s, _ = p.shape
    pf = p.astype(jnp.float32)
    cs = jnp.concatenate([jnp.zeros((b, 1, POOL_WIDTH), jnp.float32), jnp.cumsum(pf, axis=1)], axis=1)
    t = jnp.arange(s)
    outs = []
    for g, w in enumerate(POOL_SIZES):
        lo = jnp.clip(t - w // 2, 0, s)
        hi = jnp.clip(t + w // 2, 0, s)
        sl = slice(g * POOL_GROUP_DIM, (g + 1) * POOL_GROUP_DIM)
        csg = cs[:, :, sl]
        mean = (csg[:, hi] - csg[:, lo]) / (hi - lo).astype(jnp.float32)[None, :, None]
        outs.append(mean - pf[:, :, sl])
    d = jnp.stack(outs, axis=2).astype(p.dtype)
    y = jnp.einsum('bsgc,gcd->bsgd', d, w_pool).reshape(b, s, POOL_WIDTH)
    return y * pool_scale


def conv_glu_ffn(h, w_up, conv_w, conv_b, w_down):
    u = h @ w_up
    up = jnp.pad(u, ((0, 0), (1, 1), (0, 0)))
    c = conv_w[0] * up[:, :-2] + conv_w[1] * up[:, 1:-1] + conv_w[2] * up[:, 2:] + conv_b
    val, gate = jnp.split(c, 2, axis=-1)
    return (jax.nn.gelu(gate) * val) @ w_down


def setup_inputs(seed: int = 0) -> dict:
    key = jax.random.key(seed)
    ks = jax.random.split(key, 16)
    f32 = jnp.float32
    nrm = lambda k, shape, scale: jax.random.normal(k, shape, f32) * scale
    return {
        "x": nrm(ks[0], (BATCH, SEQ, D_MODEL), 1.0),
        "w_in": nrm(ks[1], (DEPTH, D_MODEL, IN_COLS), D_MODEL ** -0.5),
        "sink": nrm(ks[2], (DEPTH, N_HEADS), 0.5),
        "w_pool": nrm(ks[3], (DEPTH, N_POOL_GROUPS, POOL_GROUP_DIM, POOL_GROUP_DIM), POOL_GROUP_DIM ** -0.5),
        "pool_scale": 1.0 + nrm(ks[4], (DEPTH, POOL_WIDTH), 0.02),
        "w_out": nrm(ks[5], (DEPTH, D_MIX, D_MODEL), DEEPNORM_BETA * D_MIX ** -0.5),
        "ln1_g": 1.0 + nrm(ks[6], (DEPTH, D_MODEL), 0.02),
        "ln1_b": nrm(ks[7], (DEPTH, D_MODEL), 0.02),
        "w_up": nrm(ks[8], (DEPTH, D_MODEL, 2 * D_FF), D_MODEL ** -0.5),
        "conv_w": nrm(ks[9], (DEPTH, CONV_WIDTH, 2 * D_FF), CONV_WIDTH ** -0.5),
        "conv_b": nrm(ks[10], (DEPTH, 2 * D_FF), 0.01),
        "w_down": nrm(ks[11], (DEPTH, D_FF, D_MODEL), DEEPNORM_BETA * D_FF ** -0.5),
        "ln2_g": 1.0 + nrm(ks[12], (DEPTH, D_MODEL), 0.02),
        "ln2_b": nrm(ks[13], (DEPTH, D_MODEL), 0.02),
        "rel_bias": nrm(ks[14], (N_BUCKETS, N_HEADS), 0.5),
    }


def reference(x, w_in, sink, w_pool, pool_scale, w_out, ln1_g, ln1_b,
              w_up, conv_w, conv_b, w_down, ln2_g, ln2_b, rel_bias):
    b, s, _ = x.shape
    pos_bias, mask = banded_bias_and_mask(rel_bias, s)
    for l in range(DEPTH):
        proj = x @ w_in[l]
        q = proj[..., :Q_COLS].reshape(b, s, N_HEADS, HEAD_DIM)
        k = proj[..., Q_COLS:Q_COLS + KV_COLS].reshape(b, s, N_KV_HEADS, HEAD_DIM)
        v = proj[..., Q_COLS + KV_COLS:Q_COLS + 2 * KV_COLS].reshape(b, s, N_KV_HEADS, HEAD_DIM)
        p = proj[..., Q_COLS + 2 * KV_COLS:]
        attn = windowed_gqa(q, k, v, sink[l], pos_bias, mask)
        pool = multiscale_pool(p, w_pool[l], pool_scale[l])
        mix = jnp.concatenate([attn, pool], axis=-1) @ w_out[l]
        x = layer_norm(DEEPNORM_ALPHA * x + mix, ln1_g[l], ln1_b[l])
        ffn = conv_glu_ffn(x, w_up[l], conv_w[l], conv_b[l], w_down[l])
        x = layer_norm(DEEPNORM_ALPHA * x + ffn, ln2_g[l], ln2_b[l])
    return x
```

```python
import math
from contextlib import ExitStack

import numpy as np
import concourse.bass as bass
import concourse.mybir as mybir
from concourse.bass_utils import run_bass_kernel_spmd

F32 = mybir.dt.float32
BF16 = mybir.dt.bfloat16
AF = mybir.ActivationFunctionType
ALU = mybir.AluOpType

D = 2048
DEPTH = 4
SEQ = 4096
NH = 16
DFF = 5504
NJ = DFF // 128
ALPHA = (2 * DEPTH) ** 0.25
EPS = 1e-5
NEG = -30000.0
POOL_SIZES = (2, 4, 8, 16)


class Op:
    __slots__ = ("eng", "fn", "deps", "is_dma", "inc", "token", "dma_n")

    def __init__(self, eng, fn, is_dma):
        self.eng = eng
        self.fn = fn
        self.deps = []
        self.is_dma = is_dma
        self.inc = False
        self.token = None
        self.dma_n = None


class _Rec:
    def __init__(self):
        self.call = None

    def __getattr__(self, name):
        def f(*args, **kw):
            self.call = (name, args, kw)
            return None
        return f


class Sched:
    ENGS = ("pe", "act", "dve", "pool", "sp")
    ND = 12

    def __init__(self, nc):
        self.nc = nc
        self.ops = {e: [] for e in self.ENGS}
        self.keys = {}
        self.ndma = {e: 0 for e in self.ENGS}

    def _track(self, op, reads, writes):
        for k in reads:
            st = self.keys.setdefault(k, [{}, []])
            for e, w in st[0].items():
                op.deps.append(w)
            st[1].append(op)
        for k in writes:
            st = self.keys.setdefault(k, [{}, []])
            for e, w in st[0].items():
                if w.is_dma or op.is_dma or e != op.eng:
                    op.deps.append(w)
            for r in st[1]:
                if r is op:
                    continue
                if r.is_dma or op.is_dma or r.eng != op.eng:
                    op.deps.append(r)
            st[0] = {(("dma%d" % id(op)) if op.is_dma else op.eng): op}
            st[1] = []

    def op(self, eng, fn, reads=(), writes=()):
        rec = _Rec()
        fn(rec)
        assert rec.call is not None
        o = Op(eng, rec.call, False)
        self._track(o, reads, writes)
        self.ops[eng].append(o)
        return o

    def dma(self, eng, out, in_, reads=(), writes=()):
        o = Op(eng, ("dma_start", (), dict(out=out, in_=in_)), True)
        o.dma_n = self.ndma[eng]
        self.ndma[eng] += 1
        self._track(o, reads, writes)
        self.ops[eng].append(o)
        return o

    def barrier(self):
        lasts = []
        for e in self.ENGS:
            seen_compute = False
            nd = 0
            for o in reversed(self.ops[e]):
                if o.is_dma:
                    if nd < self.ND:
                        lasts.append(o)
                    nd += 1
                elif not seen_compute and o.fn is not None:
                    lasts.append(o)
                    seen_compute = True
                if seen_compute and nd >= self.ND:
                    break
        for e in self.ENGS:
            b = Op(e, None, False)
            b.deps = list(lasts)
            self.ops[e].append(b)
        self.keys = {}

    def emit(self, stack):
        nc = self.nc
        sem = {e: stack.enter_context(nc.semaphore("s_" + e)) for e in self.ENGS}
        dsem = {e: [stack.enter_context(nc.semaphore("d_%s%d" % (e, i))) for i in range(self.ND)]
                for e in self.ENGS if self.ndma[e] > 0}
        for e in self.ENGS:
            for o in self.ops[e]:
                for d in o.deps:
                    d.inc = True
        for e in self.ENGS:
            c = 0
            for o in self.ops[e]:
                if o.is_dma:
                    o.token = (dsem[e][o.dma_n % self.ND], 16 * (o.dma_n // self.ND + 1))
                elif o.inc and o.fn is not None:
                    c += 1
                    o.token = (sem[e], c)
        engmap = {"pe": "tensor", "act": "scalar", "dve": "vector", "pool": "gpsimd", "sp": "sync"}
        block = stack.enter_context(nc.Block())
        for e in self.ENGS:
            ops = self.ops[e]
            if not ops:
                continue

            def body(eng, ops=ops, e=e):
                seen = {}
                for o in ops:
                    need = {}
                    for d in o.deps:
                        if d.token is None:
                            continue
                        s, v = d.token
                        if seen.get(id(s), 0) >= v:
                            continue
                        if need.get(id(s), (None, 0))[1] < v:
                            need[id(s)] = (s, v)
                    if o.is_dma and o.dma_n >= self.ND:
                        s = dsem[e][o.dma_n % self.ND]
                        v = 16 * (o.dma_n // self.ND)
                        if seen.get(id(s), 0) < v and need.get(id(s), (None, 0))[1] < v:
                            need[id(s)] = (s, v)
                    for k, (s, v) in need.items():
                        eng.wait_ge(s, v)
                        seen[k] = v
                    if o.fn is None:
                        continue
                    name, args, kw = o.fn
                    ins = getattr(eng, name)(*args, **kw)
                    if o.is_dma:
                        ins.then_inc(o.token[0], 16)
                    elif o.inc:
                        ins.then_inc(o.token[0], 1)

            getattr(block, engmap[e])(body)


class Arena:
    def __init__(self, t, nwords):
        self.t = t
        self.n = nwords
        self.off = 0

    def _take(self, words):
        a = self.off
        self.off += words
        assert self.off <= self.n, "arena overflow %d > %d" % (self.off, self.n)
        return self.t[:, a:a + words]

    def f32(self, *shape):
        w = int(np.prod(shape))
        ap = self._take(w)
        if len(shape) == 2:
            ap = ap.rearrange("p (a b) -> p a b", a=shape[0])
        elif len(shape) == 3:
            ap = ap.rearrange("p (a b c) -> p a b c", a=shape[0], b=shape[1])
        return ap

    def bf16(self, *shape):
        n = int(np.prod(shape))
        w = (n + 1) // 2
        ap = self._take(w).bitcast(BF16)[:, 0:n]
        if len(shape) == 2:
            ap = ap.rearrange("p (a b) -> p a b", a=shape[0])
        elif len(shape) == 3:
            ap = ap.rearrange("p (a b c) -> p a b c", a=shape[0], b=shape[1])
        return ap


QC_HEADS = []
for _n in (0, 1, 4, 5):
    QC_HEADS.append((2 * _n, 2 * _n + 4))
    QC_HEADS.append((2 * _n + 1, 2 * _n + 5))
HEAD_LOC = {}
for _qc, (_a, _b) in enumerate(QC_HEADS):
    HEAD_LOC[_a] = (_qc, 0)
    HEAD_LOC[_b] = (_qc, 1)

ARENA_WORDS = 48000
NSTG = 4
KVP_BG_EVERY = 2
NSG = 3


def build_program(n_layers, T_in, n_out, layer0=0, debug=False):
    nc = bass.Bass("TRN2", target_bir_lowering=False)
    NTOK = T_in * 128
    dt_in = lambda name, shape: nc.dram_tensor(name, shape, F32, kind="ExternalInput").ap()
    x_in = dt_in("x_in", [NTOK, D])
    w_in = dt_in("w_in", [n_layers, D, 2560])
    w_out = dt_in("w_out", [n_layers, D, D])
    w_up = dt_in("w_up", [n_layers, D, 2 * DFF])
    w_down = dt_in("w_down", [n_layers, DFF, D])
    w_pool = dt_in("w_pool", [n_layers, 4, 256, 256])
    lnp = dt_in("lnp", [n_layers, 4, 128, D])
    sinkb = dt_in("sinkb", [n_layers, 128, NH])
    pscale = dt_in("pscale", [n_layers, 128, 8])
    cwin = dt_in("cw", [n_layers, 128, 2 * NJ, 4])
    biasin = dt_in("biasT", [128, NH * 384])
    pmin = dt_in("pm", [128, 16 * 128])
    identin = dt_in("ident", [128, 128])
    out = nc.dram_tensor("out", [n_out, D], F32, kind="ExternalOutput").ap()
    X1 = nc.dram_tensor("X1", [NTOK, D], F32, kind="ExternalOutput" if debug else "Internal").ap()
    X2 = nc.dram_tensor("X2", [NTOK, D], F32, kind="Internal").ap()
    WC_TOTAL = 16 * (2560 + 2048 + 2 * DFF) + NJ * 2048
    wc = nc.dram_tensor("wc", [2, 128, WC_TOTAL], BF16, kind="Internal").ap()

    with ExitStack() as st:
        arena_t = st.enter_context(nc.sbuf_tensor("arena", [128, ARENA_WORDS], F32))
        ps = st.enter_context(nc.psum_tensor("ps", [128, 8, 512], F32))
        S = Sched(nc)
        A = Arena(arena_t, ARENA_WORDS)

        ident = A.f32(128)
        mn_bf = A.bf16(12, 128)
        m0hi = A.bf16(4, 128)
        m0lo = A.bf16(4, 128)
        mhalf = A.f32(1)
        epsc = A.f32(1)
        base_mark = A.off

        S.dma("sp", ident, identin, writes=["ident"])
        S.op("dve", lambda e: e.memset(mhalf, -0.5), writes=["mhalf"])
        S.op("dve", lambda e: e.memset(epsc, EPS), writes=["epsc"])
        tmp_pm = A.f32(16, 128)
        tmp_d = A.f32(4, 128)
        S.dma("sp", tmp_pm, pmin.rearrange("p (a b) -> p a b", a=16), writes=["tmp_pm"])
        S.op("dve", lambda e: e.tensor_copy(out=mn_bf, in_=tmp_pm[:, 0:12, :]), reads=["tmp_pm"], writes=["mn"])
        S.op("dve", lambda e: e.tensor_copy(out=m0hi, in_=tmp_pm[:, 12:16, :]), reads=["tmp_pm"], writes=["m0hi"])
        S.op("dve", lambda e: e.tensor_tensor(out=tmp_d, in0=tmp_pm[:, 12:16, :], in1=m0hi, op=ALU.subtract),
             reads=["tmp_pm", "m0hi"], writes=["tmp_d"])
        S.op("dve", lambda e: e.tensor_copy(out=m0lo, in_=tmp_d), reads=["tmp_d"], writes=["m0lo"])
        S.barrier()
        A.off = base_mark

        class Ctx:
            pass

        def dump(name, ap, reads):
            if not debug:
                return
            shp = list(ap.shape)
            dt = nc.dram_tensor("dbg_" + name, shp, ap.dtype, kind="ExternalOutput").ap()
            S.dma("sp", dt, ap, reads=reads, writes=[("dbg", name)])

        C = Ctx()
        C.stg_i = 0
        C.w4_i = 0
        C.ev_i = 0

        CAST_PAT = ("act", "dve", "act", "dve", "pool")
        C.cast_i = 0

        def cast(out_ap, in_ap, reads, writes):
            eng = CAST_PAT[C.cast_i % len(CAST_PAT)]
            C.cast_i += 1
            if eng == "act":
                S.op("act", lambda e: e.activation(out=out_ap, in_=in_ap, func=AF.Copy), reads=reads, writes=writes)
            else:
                S.op(eng, lambda e: e.tensor_copy(out=out_ap, in_=in_ap), reads=reads, writes=writes)

        C.wc_off = {}
        C.wc_next = 0

        C.par = 0

        def wc_view(piece, shape, par=None):
            if par is None:
                par = C.par
            n = int(np.prod(shape))
            if piece not in C.wc_off:
                C.wc_off[piece] = (C.wc_next, n)
                C.wc_next += n
                assert C.wc_next <= WC_TOTAL
            off, n0 = C.wc_off[piece]
            assert n0 == n
            v = wc[par][:, off:off + n]
            if len(shape) == 2:
                v = v.rearrange("p (a b) -> p a b", a=shape[0])
            return v

        def wc_store(piece, shape, src_bf, src_keys):
            S.dma("pool", wc_view(piece, shape), src_bf, reads=src_keys, writes=[("wc", piece)])

        def wc_load(piece, shape, dst_bf, dst_keys):
            S.dma("sp", dst_bf, wc_view(piece, shape), reads=[("wc", piece)], writes=dst_keys)

        def load_w(dst_bf, dst_key, src_ap, shape, stg, piece=None, first=True):
            if piece is not None and not first:
                wc_load(piece, shape, dst_bf, [dst_key])
                return
            i = stg_next()
            sview = C.stg[i]
            sv = sview[:, 0:shape[0] * shape[1]].rearrange("p (a b) -> p a b", a=shape[0])
            S.dma("sp", sv, src_ap, writes=[("stg", i)])
            cast(dst_bf, sv, [("stg", i)], [dst_key])
            if piece is not None:
                wc_store(piece, shape, dst_bf, [dst_key])

        def evac_copy(out_ap, in_ap, reads, writes, scale=None):
            C.ev_i += 1
            if scale is not None:
                S.op("act", lambda e: e.activation(out=out_ap, in_=in_ap, func=AF.Copy, scale=scale),
                     reads=reads, writes=writes)
            elif C.ev_i % 2:
                S.op("act", lambda e: e.activation(out=out_ap, in_=in_ap, func=AF.Copy), reads=reads, writes=writes)
            else:
                S.op("dve", lambda e: e.tensor_copy(out=out_ap, in_=in_ap), reads=reads, writes=writes)

        def transpose_rows(src_sb, src_key, n, dst3, dst_key_fn, col0, tbank0):
            for c4 in range(4):
                bank = tbank0 + (c4 % 2)
                for c in range(4):
                    cc = c4 * 4 + c
                    S.op("pe", lambda e, cc=cc, c=c, bank=bank: e.transpose(
                        ps[:, bank, c * 128:c * 128 + n], src_sb[0:n, cc * 128:(cc + 1) * 128], ident[0:n, 0:n]),
                        reads=[src_key, "ident"], writes=[("ps", bank)])
                evac_copy(dst3[:, c4 * 4:c4 * 4 + 4, col0:col0 + n],
                          ps[:, bank, :].rearrange("p (a b) -> p a b", a=4)[:, :, 0:n],
                          reads=[("ps", bank)], writes=[dst_key_fn(c4)])

        def layer_norm_multi(items, g_t, b_t, small):
            stt, mv, sc = small
            for (xr, key, n, dst_ap, dst_key, slot) in items:
                for q in range(4):
                    S.op("dve", lambda e: e.bn_stats(out=stt[0:n, slot, q, :], in_=xr[0:n, q * 512:(q + 1) * 512]),
                         reads=[key], writes=[("stt", slot)])
                S.op("dve", lambda e: e.bn_aggr(out=mv[0:n, slot, :], in_=stt[0:n, slot, :, :]),
                     reads=[("stt", slot)], writes=[("mv", slot)])
                S.op("dve", lambda e: e.tensor_scalar(out=sc[0:n, slot, 0:1], in0=mv[0:n, slot, 1:2], scalar1=EPS,
                                                      scalar2=None, op0=ALU.add),
                     reads=[("mv", slot)], writes=[("sc0", slot)])
            for (xr, key, n, dst_ap, dst_key, slot) in items:
                S.op("pool", lambda e: e.tensor_tensor(out=sc[0:n, slot, 1:2], in0=sc[0:n, slot, 0:1], in1=mhalf[0:n, :],
                                                       op=ALU.pow), reads=[("sc0", slot), "mhalf"], writes=[("sc1", slot)])
            for (xr, key, n, dst_ap, dst_key, slot) in items:
                S.op("dve", lambda e: e.tensor_scalar(out=sc[0:n, slot, 2:3], in0=mv[0:n, slot, 0:1], scalar1=-1.0,
                                                      scalar2=sc[0:n, slot, 1:2], op0=ALU.mult, op1=ALU.mult),
                     reads=[("mv", slot), ("sc1", slot)], writes=[("sc2", slot)])
            for (xr, key, n, dst_ap, dst_key, slot) in items:
                S.op("act", lambda e: e.activation(out=xr[0:n, :], in_=xr[0:n, :], func=AF.Identity,
                                                   bias=sc[0:n, slot, 2:3], scale=sc[0:n, slot, 1:2]),
                     reads=[key, ("sc1", slot), ("sc2", slot)], writes=[key])
            for (xr, key, n, dst_ap, dst_key, slot) in items:
                S.op("dve", lambda e: e.tensor_tensor(out=xr[0:n, :], in0=xr[0:n, :], in1=g_t[0:n, :], op=ALU.mult),
                     reads=[key, "lng"], writes=[key])
            for (xr, key, n, dst_ap, dst_key, slot) in items:
                S.op("pool", lambda e: e.tensor_tensor(out=xr[0:n, :], in0=xr[0:n, :], in1=b_t[0:n, :], op=ALU.add),
                     reads=[key, "lnb"], writes=[key])
                S.dma("pool", dst_ap, xr[0:n, :], reads=[key], writes=[dst_key])

        def run_pipeline(steps, PF):
            n = len(steps)
            state = [None] * n
            for i in range(n + PF):
                if i < n:
                    state[i] = steps[i][0]()
                if i >= PF:
                    steps[i - PF][1](state[i - PF])

        def stg_next():
            i = C.stg_i % len(C.stg)
            C.stg_i += 1
            return i

        def load_x_T(src_rows_ap, n, src_keys, dst3, dst_key_fn, col0):
            i = stg_next()
            S.dma("sp", C.stg[i][0:n, :], src_rows_ap, reads=src_keys, writes=[("stg", i)])
            transpose_rows(C.stg[i], ("stg", i), n, dst3, dst_key_fn, col0, 4)

        for L in range(n_layers):
            T_KV = T_in - L
            T_A = T_KV - 1
            last = (L == n_layers - 1)
            xsrc = x_in if L == 0 else X2
            xkey = (lambda t: ("X2", t)) if L > 0 else (lambda t: ("xin", t))
            win_v = w_in[L].rearrange("(k p) c -> p k c", p=128)
            C.par = L % 2
            bg_jobs = []
            if L + 1 < n_layers:
                Ln = L + 1
                wn = w_in[Ln].rearrange("(k p) c -> p k c", p=128)
                for c in range(2):
                    bg_jobs.append((("k", c), (16, 128), [(None, wn[:, :, 1024 + c * 128:1024 + (c + 1) * 128])]))
                for qc, (ha, hb) in enumerate(QC_HEADS):
                    bg_jobs.append((("q", qc), (16, 128), [((0, 64), wn[:, :, ha * 64:(ha + 1) * 64]),
                                                           ((64, 128), wn[:, :, hb * 64:(hb + 1) * 64])]))
                for part in range(5):
                    col0 = 1280 if part == 0 else 1536 + (part - 1) * 256
                    for hh in range(2):
                        bg_jobs.append((("vp", part, hh), (16, 128),
                                        [(None, wn[:, :, col0 + hh * 128:col0 + (hh + 1) * 128])]))
                for cb in range(4):
                    for k4 in range(4):
                        bg_jobs.append((("o", cb, k4), (4, 512), [(None, w_out[Ln][
                            k4 * 512:(k4 + 1) * 512, cb * 512:(cb + 1) * 512].rearrange("(k p) c -> p k c", p=128))]))
                wun = w_up[Ln].rearrange("(k p) c -> p k c", p=128)
                for j in range(NJ):
                    for vg in range(2):
                        bg_jobs.append((("u", vg, j), (16, 128),
                                        [(None, wun[:, :, vg * DFF + j * 128:vg * DFF + (j + 1) * 128])]))
                for cb in range(4):
                    for h4 in range(11):
                        nk = min(4, NJ - h4 * 4)
                        bg_jobs.append((("d", cb, h4), (nk, 512), [(None, w_down[Ln][
                            h4 * 512:h4 * 512 + nk * 128, cb * 512:(cb + 1) * 512].rearrange("(k p) c -> p k c", p=128))]))
            bg_state = {"pending": None, "t": 0, "tick": 0, "stg": None, "bf": None, "every": 2}

            def bg_finish():
                if bg_state["pending"] is not None:
                    piece, shape, slot = bg_state["pending"]
                    n = int(np.prod(shape))
                    sv = bg_state["stg"][slot][:, 0:n].rearrange("p (a b) -> p a b", a=shape[0])
                    bv = bg_state["bf"][slot][:, 0:n].rearrange("p (a b) -> p a b", a=shape[0])
                    cast(bv, sv, [("sbg", slot)], [("bbg", slot)])
                    S.dma("pool", wc_view(piece, shape, (L + 1) % 2), bv, reads=[("bbg", slot)],
                          writes=[("wcn", piece)])
                    bg_state["pending"] = None

            def bg_tick(force=False):
                bg_state["tick"] += 1
                if not force and bg_state["tick"] % bg_state["every"]:
                    return
                bg_finish()
                if bg_jobs:
                    piece, shape, loads = bg_jobs.pop(0)
                    slot = bg_state["t"] % len(bg_state["stg"])
                    bg_state["t"] += 1
                    n = int(np.prod(shape))
                    sv = bg_state["stg"][slot][:, 0:n].rearrange("p (a b) -> p a b", a=shape[0])
                    for sub, src in loads:
                        dstv = sv if sub is None else sv[:, :, sub[0]:sub[1]]
                        S.dma("sp", dstv, src, writes=[("sbg", slot)])
                    bg_state["pending"] = (piece, shape, slot)

            nsg = NSG
            bnds = [(T_A * i) // nsg for i in range(nsg + 1)]
            sgroups = [(bnds[i], bnds[i + 1]) for i in range(nsg)]
            for (ta, tb) in sgroups:
                if tb <= ta:
                    continue
                kv0 = max(ta - 1, 0)
                kv1 = tb + 1
                nres = kv1 - kv0
                A.off = base_mark
                KT = A.bf16(2, nres * 128)
                Vx = A.bf16(nres, 4, 65)
                Pres = A.bf16(nres, 1024)
                nq = tb - ta
                QTres = A.bf16(8, nq * 128)
                res_mark = A.off
                bufA = A.bf16(16, 512)
                wbig = [A.bf16(16, 256) for _ in range(2)]
                w4 = [A.bf16(2048) for _ in range(4)]
                C.stg = [A.f32(2048) for _ in range(NSTG)]
                stg = C.stg
                bg_state["stg"] = [A.f32(2048) for _ in range(2)]
                bg_state["bf"] = [A.bf16(2048) for _ in range(2)]
                bg_state["every"] = KVP_BG_EVERY
                bg_state["t"] = 0
                S.op("dve", lambda e: e.memset(Vx[:, :, :, 64:65], 1.0), writes=["Vones"])
                tiles = list(range(kv0, kv1))
                for g0 in range(0, len(tiles), 4):
                    grp = tiles[g0:g0 + 4]
                    ng = len(grp)
                    ntok = ng * 128
                    kvp_first = (L == 0 and ta == 0 and g0 == 0)
                    for ti, t in enumerate(grp):
                        load_x_T(xsrc[t * 128:(t + 1) * 128, :], 128, [xkey(t)], bufA, lambda c4: ("bufA", c4), ti * 128)
                    bufA_keys = [("bufA", c4) for c4 in range(4)]
                    steps = []

                    def k_prep(c):
                        wi = C.w4_i % 4
                        C.w4_i += 1
                        wk = w4[wi].rearrange("p (a b) -> p a b", a=16)
                        load_w(wk, ("w4", wi), win_v[:, :, 1024 + c * 128:1024 + (c + 1) * 128], (16, 128), stg,
                               piece=("k", c), first=kvp_first)
                        return (wi, wk)

                    def k_run(stt_, c, grp=grp, ntok=ntok):
                        wi, wk = stt_
                        bg_tick()
                        bank = c % 2
                        for kc in range(16):
                            S.op("pe", lambda e: e.matmul(
                                ps[:, bank, 0:ntok], lhsT=wk[:, kc, :], rhs=bufA[:, kc, 0:ntok],
                                start=(kc == 0), stop=(kc == 15)),
                                reads=[("w4", wi)] + bufA_keys, writes=[("ps", bank)])
                        s0 = (grp[0] - kv0) * 128
                        evac_copy(KT[:, c, s0:s0 + ntok], ps[:, bank, 0:ntok], reads=[("ps", bank)],
                                  writes=[("KT", c, t - kv0) for t in grp])

                    def p_prep(part):
                        col0 = 1280 if part == 0 else 1536 + (part - 1) * 256
                        wb_i = part % 2
                        wb = wbig[wb_i]
                        for hh in range(2):
                            load_w(wb[:, :, hh * 128:(hh + 1) * 128], ("wbig", wb_i, hh),
                                   win_v[:, :, col0 + hh * 128:col0 + (hh + 1) * 128], (16, 128), stg,
                                   piece=("vp", part, hh), first=kvp_first)
                        return (wb_i, wb)

                    def p_run(stt_, part, grp=grp):
                        wb_i, wb = stt_
                        bg_tick()
                        for ti, t in enumerate(grp):
                            bank = ti
                            for kc in range(16):
                                S.op("pe", lambda e: e.matmul(
                                    ps[:, bank, 0:256], lhsT=bufA[:, kc, ti * 128:(ti + 1) * 128], rhs=wb[:, kc, :],
                                    start=(kc == 0), stop=(kc == 15)),
                                    reads=[("wbig", wb_i, 0), ("wbig", wb_i, 1)] + bufA_keys, writes=[("ps", bank)])
                            sl = t - kv0
                            if part == 0:
                                evac_copy(Vx[:, sl, :, 0:64], ps[:, bank, 0:256].rearrange("p (a b) -> p a b", a=4),
                                          reads=[("ps", bank)], writes=[("V", sl)])
                            else:
                                evac_copy(Pres[:, sl, (part - 1) * 256:part * 256], ps[:, bank, 0:256],
                                          reads=[("ps", bank)], writes=[("P", sl, part - 1)])

                    qlo = max(grp[0], ta)
                    qhi = min(grp[-1] + 1, tb)
                    def q_prep(pi, n):
                        if not kvp_first:
                            outs = []
                            for ab in range(2):
                                wi = C.w4_i % 4
                                C.w4_i += 1
                                wq = w4[wi].rearrange("p (a b) -> p a b", a=16)
                                wc_load(("q", pi * 2 + ab), (16, 128), wq, [("w4", wi)])
                                outs.append((wi, wq))
                            return outs
                        ia = stg_next()
                        ib = stg_next()
                        sa = stg[ia].rearrange("p (a b) -> p a b", a=16)
                        sbb = stg[ib].rearrange("p (a b) -> p a b", a=16)
                        S.dma("sp", sa, win_v[:, :, n * 128:(n + 1) * 128], writes=[("stg", ia)])
                        S.dma("sp", sbb, win_v[:, :, (n + 2) * 128:(n + 3) * 128], writes=[("stg", ib)])
                        outs = []
                        for ab in range(2):
                            wi = C.w4_i % 4
                            C.w4_i += 1
                            wq = w4[wi].rearrange("p (a b) -> p a b", a=16)
                            C.cast_i = 0 if ab == 0 else 1
                            cast(wq[:, :, 0:64], sa[:, :, ab * 64:(ab + 1) * 64], [("stg", ia)], [("w4", wi)])
                            C.cast_i = 0 if ab == 0 else 1
                            cast(wq[:, :, 64:128], sbb[:, :, ab * 64:(ab + 1) * 64], [("stg", ib)], [("w4", wi)])
                            wc_store(("q", pi * 2 + ab), (16, 128), wq, [("w4", wi)])
                            outs.append((wi, wq))
                        return outs

                    def q_run(outs, pi, qlo=qlo, qhi=qhi, grp=grp):
                        bg_tick()
                        c_lo = (qlo - grp[0]) * 128
                        c_hi = (qhi - grp[0]) * 128
                        nn = c_hi - c_lo
                        for ab in range(2):
                            wi, wq = outs[ab]
                            qc = pi * 2 + ab
                            bank = qc % 4
                            for kc in range(16):
                                S.op("pe", lambda e: e.matmul(
                                    ps[:, bank, 0:nn], lhsT=wq[:, kc, :], rhs=bufA[:, kc, c_lo:c_hi],
                                    start=(kc == 0), stop=(kc == 15)),
                                    reads=[("w4", wi)] + bufA_keys, writes=[("ps", bank)])
                            evac_copy(QTres[:, qc, (qlo - ta) * 128:(qhi - ta) * 128], ps[:, bank, 0:nn], reads=[("ps", bank)],
                                      writes=[("QT", qc, t - ta) for t in range(qlo, qhi)], scale=0.125)


                    for c in range(2):
                        steps.append((lambda c=c: k_prep(c), lambda s_, c=c: k_run(s_, c)))
                    if qhi > qlo:
                        for pi, n in enumerate((0, 1, 4, 5)):
                            steps.append((lambda pi=pi, n=n: q_prep(pi, n), lambda o_, pi=pi: q_run(o_, pi)))
                    for part in range(5):
                        steps.append((lambda part=part: p_prep(part), lambda s_, part=part: p_run(s_, part)))
                    run_pipeline(steps, 1)
                bg_finish()
                S.barrier()

                A.off = res_mark
                biasT = A.bf16(NH, 384)
                bufA = A.bf16(16, 512)
                w4 = [A.bf16(2048) for _ in range(4)]
                pT = [A.bf16(384) for _ in range(3)]
                DT = A.bf16(8, 128)
                wpool_sb = A.bf16(4, 2, 256)
                C.stg = [A.f32(2048) for _ in range(NSTG)]
                stg = C.stg
                xr = [A.f32(2048) for _ in range(4)]
                g_t = A.f32(2048)
                b_t = A.f32(2048)
                sb = [A.f32(384) for _ in range(2)]
                attn = A.f32(1024)
                esink = A.f32(NH)
                den = A.f32(NH)
                rec = A.f32(NH)
                psc = A.f32(8)
                stt = A.f32(4, 4, 6)
                mv = A.f32(4, 2)
                sc = A.f32(4, 4)
                small = (stt, mv, sc)
                S.dma("sp", g_t, lnp[L, 0], writes=["lng"])
                S.dma("sp", b_t, lnp[L, 1], writes=["lnb"])
                S.dma("sp", esink, sinkb[L], writes=["esink"])
                S.op("act", lambda e: e.activation(out=esink, in_=esink, func=AF.Exp), reads=["esink"], writes=["esink"])
                S.dma("sp", psc, pscale[L], writes=["psc"])
                bflat = biasT.rearrange("p a b -> p (a b)")
                for i3 in range(3):
                    i = stg_next()
                    S.dma("sp", stg[i], biasin[:, i3 * 2048:(i3 + 1) * 2048], writes=[("stg", i)])
                    C.cast_i = i3
                    cast(bflat[:, i3 * 2048:(i3 + 1) * 2048], stg[i], [("stg", i)], [("biasT", i3)])
                i = stg_next()
                wp_st = stg[i].rearrange("p (g c d) -> p g c d", g=4, c=2)
                S.dma("sp", wp_st, w_pool[L].rearrange("g (c p) d -> p g c d", p=128), writes=[("stg", i)])
                S.op("dve", lambda e: e.tensor_copy(out=wpool_sb, in_=wp_st), reads=[("stg", i)], writes=["wpool"])

                qtiles = list(range(ta, tb))
                for g0 in range(0, len(qtiles), 4):
                    grp = qtiles[g0:g0 + 4]
                    ng = len(grp)
                    ntok = ng * 128
                    bufA_keys = [("bufA", c4) for c4 in range(4)]
                    mix_first = (L == 0 and ta == 0 and g0 == 0)
                    for ti, t in enumerate(grp):
                        sl = t - kv0
                        blocks = [b for b in range(3) if t + b - 1 >= 0]
                        c0 = blocks[0] * 128
                        for r in range(2):
                            for c in range(4):
                                c8 = r * 4 + c
                                g = c8 // 2
                                mats = []
                                if t == 0:
                                    mats.append((sl, m0hi[:, g, :], "m0hi"))
                                    mats.append((sl, m0lo[:, g, :], "m0lo"))
                                    mats.append((sl + 1, mn_bf[:, g * 3 + 2, :], "mn"))
                                else:
                                    for b in range(3):
                                        mats.append((sl + b - 1, mn_bf[:, g * 3 + b, :], "mn"))
                                for mi, (slp, mat, mkey) in enumerate(mats):
                                    S.op("pe", lambda e: e.matmul(
                                        ps[:, 6, c * 128:(c + 1) * 128], lhsT=Pres[:, slp, c8 * 128:(c8 + 1) * 128], rhs=mat,
                                        start=(mi == 0), stop=(mi == len(mats) - 1)),
                                        reads=[("P", slp, c8 // 2), mkey], writes=[("ps", 6)])
                            S.op("dve", lambda e: e.tensor_copy(
                                out=DT[:, r * 4:r * 4 + 4, :], in_=ps[:, 6, :].rearrange("p (a b) -> p a b", a=4)),
                                reads=[("ps", 6)], writes=[("DT", r)])
                        for r in range(2):
                            for c in range(4):
                                dc = r * 4 + c
                                g = dc // 2
                                dd = dc % 2
                                for cc in range(2):
                                    S.op("pe", lambda e: e.matmul(
                                        ps[:, 7, c * 128:(c + 1) * 128], lhsT=wpool_sb[:, g, cc, dd * 128:(dd + 1) * 128],
                                        rhs=DT[:, 2 * g + cc, :], start=(cc == 0), stop=(cc == 1)),
                                        reads=["wpool", ("DT", g // 2)], writes=[("ps", 7)])
                            for c in range(4):
                                dc = r * 4 + c
                                S.op("act", lambda e: e.activation(
                                    out=bufA[:, 8 + dc, ti * 128:(ti + 1) * 128], in_=ps[:, 7, c * 128:(c + 1) * 128],
                                    func=AF.Identity, scale=psc[:, dc:dc + 1]),
                                    reads=[("ps", 7), "psc"], writes=[("bufA", 2 + r)])

                        def s_mm(h):
                            qc, hf = HEAD_LOC[h]
                            kvc = (h // 4) // 2
                            sbank = 4 + h % 2
                            for b in blocks:
                                S.op("pe", lambda e: e.matmul(
                                    ps[:, sbank, b * 128:(b + 1) * 128],
                                    lhsT=KT[hf * 64:(hf + 1) * 64, kvc, (sl + b - 1) * 128:(sl + b) * 128],
                                    rhs=QTres[hf * 64:(hf + 1) * 64, qc, (t - ta) * 128:(t - ta + 1) * 128], start=True, stop=True),
                                    reads=[("KT", kvc, sl + b - 1), ("QT", qc, t - ta)], writes=[("ps", sbank)])

                        def soft(h):
                            sbank = 4 + h % 2
                            sbi = h % 2
                            pti = h % 3
                            S.op("dve", lambda e: e.tensor_tensor(
                                out=sb[sbi][:, c0:384], in0=ps[:, sbank, c0:384], in1=biasT[:, h, c0:384], op=ALU.add),
                                reads=[("ps", sbank)] + [("biasT", i3) for i3 in range(3)], writes=[("sb", sbi)])
                            S.op("act", lambda e: e.activation(
                                out=pT[pti][:, c0:384], in_=sb[sbi][:, c0:384], func=AF.Exp),
                                reads=[("sb", sbi)], writes=[("pT", pti)])

                        def pv(h):
                            kv = h // 4
                            pti = h % 3
                            obank = h // 7
                            oslot = (h % 7) * 65
                            for b in blocks:
                                S.op("pe", lambda e: e.matmul(
                                    ps[:, obank, oslot:oslot + 65], lhsT=pT[pti][:, b * 128:(b + 1) * 128],
                                    rhs=Vx[:, sl + b - 1, kv, :], start=(b == blocks[0]), stop=(b == blocks[-1])),
                                    reads=[("pT", pti), ("V", sl + b - 1), "Vones"], writes=[("ps", obank)])

                        s_mm(0)
                        for h in range(NH):
                            if h + 1 < NH:
                                s_mm(h + 1)
                            soft(h)
                            pv(h)
                        for obank in range(3):
                            h0 = obank * 7
                            nh = min(7, NH - h0)
                            pso = ps[:, obank, 0:nh * 65].rearrange("p (a b) -> p a b", a=nh)
                            S.op("dve", lambda e: e.tensor_tensor(
                                out=den[:, h0:h0 + nh].unsqueeze(2), in0=pso[:, :, 64:65],
                                in1=esink[:, h0:h0 + nh].unsqueeze(2), op=ALU.add),
                                reads=[("ps", obank), "esink"], writes=[("den", obank)])
                            S.op("dve", lambda e: e.reciprocal(out=rec[:, h0:h0 + nh], in_=den[:, h0:h0 + nh]),
                                 reads=[("den", obank)], writes=[("rec", obank)])
                            S.op("dve", lambda e: e.tensor_tensor(
                                out=attn[:, h0 * 64:(h0 + nh) * 64].rearrange("p (a b) -> p a b", a=nh),
                                in0=pso[:, :, 0:64],
                                in1=rec[:, h0:h0 + nh].unsqueeze(2).to_broadcast([128, nh, 64]), op=ALU.mult),
                                reads=[("ps", obank), ("rec", obank)], writes=[("attn", obank)])
                        for r in range(2):
                            for c in range(4):
                                cc = r * 4 + c
                                S.op("pe", lambda e: e.transpose(
                                    ps[:, 3, c * 128:(c + 1) * 128], attn[:, cc * 128:(cc + 1) * 128], ident),
                                    reads=[("attn", 0), ("attn", 1), ("attn", 2), "ident"], writes=[("ps", 3)])
                            evac_copy(bufA[:, r * 4:r * 4 + 4, ti * 128:(ti + 1) * 128],
                                      ps[:, 3, :].rearrange("p (a b) -> p a b", a=4),
                                      reads=[("ps", 3)], writes=[("bufA", r)])
                    if debug and L == 0 and ta == 0 and g0 == 0:
                        dump("attn", attn, [("attn", 0), ("attn", 1), ("attn", 2)])
                        dump("mixT", bufA, [("bufA", q) for q in range(4)])
                    wo_v = w_out[L]
                    for ti, t in enumerate(grp):
                        S.dma("sp", xr[ti], xsrc[t * 128:(t + 1) * 128, :], reads=[xkey(t)], writes=[("xr", ti)])

                    def o_prep(cb, k4):
                        wi = C.w4_i % 4
                        C.w4_i += 1
                        wo = w4[wi].rearrange("p (a b) -> p a b", a=4)
                        load_w(wo, ("w4", wi),
                               wo_v[k4 * 512:(k4 + 1) * 512, cb * 512:(cb + 1) * 512].rearrange("(k p) c -> p k c", p=128),
                               (4, 512), stg, piece=("o", cb, k4), first=mix_first)
                        return (wi, wo)

                    def o_run(s_, cb, k4, ng=ng, grp=grp):
                        wi, wo = s_
                        for ti in range(ng):
                            for kk in range(4):
                                kc = k4 * 4 + kk
                                S.op("pe", lambda e: e.matmul(
                                    ps[:, ti, :], lhsT=bufA[:, kc, ti * 128:(ti + 1) * 128], rhs=wo[:, kk, :],
                                    start=(kc == 0), stop=(kc == 15)),
                                    reads=[("w4", wi)] + bufA_keys, writes=[("ps", ti)])
                        if k4 == 3:
                            for ti in range(ng):
                                S.op("dve", lambda e: e.scalar_tensor_tensor(
                                    out=xr[ti][:, cb * 512:(cb + 1) * 512], in0=xr[ti][:, cb * 512:(cb + 1) * 512],
                                    scalar=ALPHA, in1=ps[:, ti, :], op0=ALU.mult, op1=ALU.add),
                                    reads=[("ps", ti), ("xr", ti)], writes=[("xr", ti)])
                            if cb == 3:
                                layer_norm_multi([(xr[ti], ("xr", ti), 128, X1[t * 128:(t + 1) * 128, :], ("X1", t), ti)
                                                  for ti, t in enumerate(grp)], g_t, b_t, small)

                    steps = [(lambda cb=cb, k4=k4: o_prep(cb, k4), lambda s_, cb=cb, k4=k4: o_run(s_, cb, k4))
                             for cb in range(4) for k4 in range(4)]
                    run_pipeline(steps, 2)
                S.barrier()

            A.off = base_mark
            bufX = A.bf16(16, 512)
            hT = A.bf16(NJ, 512)
            w4 = [A.bf16(2048) for _ in range(4)]
            C.stg = [A.f32(2048) for _ in range(3)]
            stg = C.stg
            stg_bg = [A.f32(2048) for _ in range(2)]
            bf_bg = [A.bf16(2048) for _ in range(2)]
            xr = [A.f32(2048) for _ in range(4)]
            g_t = A.f32(2048)
            b_t = A.f32(2048)
            tvb = [A.f32(512) for _ in range(2)]
            tgb = [A.f32(512) for _ in range(2)]
            cw = A.f32(2 * NJ, 4)
            stt = A.f32(4, 4, 6)
            mv = A.f32(4, 2)
            sc = A.f32(4, 4)
            small = (stt, mv, sc)
            S.dma("sp", g_t, lnp[L, 2], writes=["lng"])
            S.dma("sp", b_t, lnp[L, 3], writes=["lnb"])
            S.dma("sp", cw, cwin[L], writes=["cw"])
            wup_v = w_up[L].rearrange("(k p) c -> p k c", p=128)
            wd_v = w_down[L]
            bg_state["stg"] = stg_bg
            bg_state["bf"] = bf_bg
            bg_state["every"] = 2
            bg_state["t"] = 0
            x1_avail = T_A * 128
            n_ffn = n_out if last else min(T_A * 128, n_out + 129 * (n_layers - 1 - L))
            if n_ffn < T_A * 128 and not last:
                S.op("dve", lambda e: e.memset(xr[0], 0.0), writes=[("xr", 0)])
                z0 = n_ffn
                while z0 < T_A * 128:
                    zn = min(128, T_A * 128 - z0)
                    S.dma("sp", X2[z0:z0 + zn, :], xr[0][0:zn, :], reads=[("xr", 0)], writes=[("X2z", z0)])
                    z0 += zn
            WIN = 510
            nwin = (n_ffn + WIN - 1) // WIN
            WEVEN = (n_ffn + nwin - 1) // nwin
            w0 = 0
            jcount = 0
            while w0 < n_ffn:
                W = min(WEVEN, n_ffn - w0)
                NU = W + 2
                bufX_keys = [("bufX", c4) for c4 in range(4)]
                tok_lo = w0 - 1
                tok_hi = w0 + W + 1
                col = 0
                if tok_lo < 0:
                    S.op("dve", lambda e: e.memset(bufX[:, :, 0:1], 0.0), writes=bufX_keys)
                    tok_lo = 0
                    col = 1
                zero_tail = False
                if tok_hi > x1_avail:
                    tok_hi = x1_avail
                    zero_tail = True
                tk = tok_lo
                while tk < tok_hi:
                    n = min(128, tok_hi - tk)
                    load_x_T(X1[tk:tk + n, :], n, [("X1", tt) for tt in range(tk // 128, (tk + n - 1) // 128 + 1)],
                             bufX, lambda c4: ("bufX", c4), col)
                    tk += n
                    col += n
                if zero_tail:
                    S.op("dve", lambda e: e.memset(bufX[:, :, col:col + 1], 0.0), writes=bufX_keys)
                    col += 1
                assert col == NU, (col, NU)
                nsub = (W + 127) // 128
                subn = [min(128, W - s * 128) for s in range(nsub)]

                def u_prep(j):
                    wis = []
                    for vg in range(2):
                        wi = C.w4_i % 4
                        C.w4_i += 1
                        wv = w4[wi].rearrange("p (a b) -> p a b", a=16)
                        load_w(wv, ("w4", wi), wup_v[:, :, vg * DFF + j * 128:vg * DFF + (j + 1) * 128], (16, 128), stg,
                               piece=("u", vg, j), first=(L == 0 and w0 == 0))
                        wis.append((wi, wv))
                    return wis

                def u_run(wis, j, W=W, NU=NU):
                    if L > 0 or w0 > 0:
                        bg_tick()
                    pb = j % 2
                    banks = (pb, 2 + pb)
                    for vg in range(2):
                        wi, wv = wis[vg]
                        for kc in range(16):
                            S.op("pe", lambda e: e.matmul(
                                ps[:, banks[vg], 0:NU], lhsT=wv[:, kc, :], rhs=bufX[:, kc, 0:NU],
                                start=(kc == 0), stop=(kc == 15)),
                                reads=[("w4", wi)] + bufX_keys, writes=[("ps", banks[vg])])
                    tv = tvb[pb]
                    tg = tgb[pb]
                    for vg, tt, tkey in ((0, tv, ("tv", pb)), (1, tg, ("tg", pb))):
                        jj = vg * NJ + j
                        bank = banks[vg]
                        S.op("act", lambda e: e.activation(
                            out=tt[:, 0:W], in_=ps[:, bank, 0:W], func=AF.Identity,
                            bias=cw[:, jj, 3:4], scale=cw[:, jj, 0:1]),
                            reads=[("ps", bank), "cw"], writes=[tkey])
                        for k in (1, 2):
                            S.op("dve", lambda e: e.scalar_tensor_tensor(
                                out=tt[:, 0:W], in0=ps[:, bank, k:k + W], scalar=cw[:, jj, k:k + 1], in1=tt[:, 0:W],
                                op0=ALU.mult, op1=ALU.add),
                                reads=[("ps", bank), "cw", tkey], writes=[tkey])
                    S.op("act", lambda e: e.activation(out=tg[:, 0:W], in_=tg[:, 0:W], func=AF.Gelu_apprx_tanh),
                         reads=[("tg", pb)], writes=[("tg", pb)])
                    S.op("pool", lambda e: e.tensor_tensor(
                        out=hT[:, j, 0:W], in0=tg[:, 0:W], in1=tv[:, 0:W], op=ALU.mult),
                        reads=[("tg", pb), ("tv", pb)], writes=[("hT", j)])

                steps = [(lambda j=j: u_prep(j), lambda s_, j=j: u_run(s_, j)) for j in range(NJ)]
                run_pipeline(steps, 1)

                for s in range(nsub):
                    r0 = w0 + s * 128
                    S.dma("sp", xr[s][0:subn[s], :], X1[r0:r0 + subn[s], :],
                          reads=[("X1", tt) for tt in range(r0 // 128, (r0 + subn[s] - 1) // 128 + 1)], writes=[("xr", s)])

                def d_prep(cb, h4):
                    nk = min(4, NJ - h4 * 4)
                    wi = C.w4_i % 4
                    C.w4_i += 1
                    wd = w4[wi].rearrange("p (a b) -> p a b", a=4)
                    if L == 0 and w0 == 0:
                        i = stg_next()
                        sv = stg[i].rearrange("p (a b) -> p a b", a=4)
                        S.dma("sp", sv[:, 0:nk, :],
                              wd_v[h4 * 512:h4 * 512 + nk * 128, cb * 512:(cb + 1) * 512].rearrange("(k p) c -> p k c", p=128),
                              writes=[("stg", i)])
                        cast(wd[:, 0:nk, :], sv[:, 0:nk, :], [("stg", i)], [("w4", wi)])
                        wc_store(("d", cb, h4), (nk, 512), wd[:, 0:nk, :], [("w4", wi)])
                    else:
                        wc_load(("d", cb, h4), (nk, 512), wd[:, 0:nk, :], [("w4", wi)])
                    return (wi, wd, nk)

                def d_run(s_, cb, h4, nsub=nsub, subn=subn, w0=w0):
                    wi, wd, nk = s_
                    if L > 0 or w0 > 0:
                        bg_tick()
                    for s in range(nsub):
                        for kk in range(nk):
                            hc = h4 * 4 + kk
                            S.op("pe", lambda e: e.matmul(
                                ps[0:subn[s], 4 + s, :], lhsT=hT[:, hc, s * 128:s * 128 + subn[s]], rhs=wd[:, kk, :],
                                start=(hc == 0), stop=(hc == NJ - 1)),
                                reads=[("w4", wi), ("hT", hc)], writes=[("ps", 4 + s)])
                    if h4 == 10:
                        for s in range(nsub):
                            S.op("dve", lambda e: e.scalar_tensor_tensor(
                                out=xr[s][0:subn[s], cb * 512:(cb + 1) * 512], in0=xr[s][0:subn[s], cb * 512:(cb + 1) * 512],
                                scalar=ALPHA, in1=ps[0:subn[s], 4 + s, :], op0=ALU.mult, op1=ALU.add),
                                reads=[("ps", 4 + s), ("xr", s)], writes=[("xr", s)])
                        if cb == 3:
                            dstT = out if last else X2
                            layer_norm_multi([(xr[s], ("xr", s), subn[s], dstT[w0 + s * 128:w0 + s * 128 + subn[s], :],
                                               ("ow", w0 + s * 128), s) for s in range(nsub)], g_t, b_t, small)

                steps = [(lambda cb=cb, h4=h4: d_prep(cb, h4), lambda s_, cb=cb, h4=h4: d_run(s_, cb, h4))
                         for cb in range(4) for h4 in range(11)]
                run_pipeline(steps, 2)
                w0 += W
            while bg_jobs:
                bg_tick(force=True)
            bg_finish()
            S.barrier()
        S.emit(st)
    return nc


def _bucket_table():
    import jax
    import jax.numpy as jnp
    cpu = jax.devices("cpu")[0]
    with jax.default_device(cpu):
        rel = jnp.arange(-128, 129)
        half = 16
        max_exact = 8
        base = jnp.where(rel > 0, half, 0)
        n = jnp.abs(rel)
        nf = jnp.maximum(n, 1).astype(jnp.float32)
        large = max_exact + (jnp.log(nf / max_exact) / math.log(128 / max_exact) * (half - max_exact)).astype(jnp.int32)
        large = jnp.minimum(large, half - 1)
        b = base + jnp.where(n < max_exact, n, large)
        return np.asarray(b)


def _bias_table(rel_bias, mirrored, bucket):
    s = np.arange(128)[:, None, None]
    b = np.arange(3)[None, :, None]
    q = np.arange(128)[None, None, :]
    relp = (b - 1) * 128 + s - q
    valid = np.abs(relp) <= 128
    relg = -relp if mirrored else relp
    idx = bucket[np.clip(relg, -128, 128) + 128]
    tab = rel_bias[idx]
    tab = np.where(valid[..., None], tab, np.float32(NEG)).astype(np.float32)
    tab = np.transpose(tab, (0, 3, 1, 2))
    return np.ascontiguousarray(tab.reshape(128, NH * 384))


def _pool_mats(mirrored):
    pm = np.zeros((128, 16, 128), np.float32)
    for g, w in enumerate(POOL_SIZES):
        for t in range(128):
            if mirrored:
                lo, hi = t - w // 2 + 1, t + w // 2 + 1
            else:
                lo, hi = t - w // 2, t + w // 2
            for p in range(lo, hi):
                b = (p + 128) // 128
                pm[p - (b - 1) * 128, g * 3 + b, t] += 1.0 / w
            pm[t, g * 3 + 1, t] -= 1.0
            lo0 = max(lo, 0)
            cnt = hi - lo0
            for p in range(lo0, min(hi, 128)):
                pm[p, 12 + g, t] += np.float32(1.0) / np.float32(cnt)
            pm[t, 12 + g, t] -= 1.0
    return np.ascontiguousarray(pm.reshape(128, 16 * 128))


def _core_inputs(x_local, layers, mirrored, P):
    L = len(layers)
    cw = np.empty((L, 128, 2 * NJ, 4), np.float32)
    for i, l in enumerate(layers):
        taps = P["conv_w"][l][::-1] if mirrored else P["conv_w"][l]
        for k in range(3):
            cw[i, :, :, k] = taps[k].reshape(2 * NJ, 128).T
        cw[i, :, :, 3] = P["conv_b"][l].reshape(2 * NJ, 128).T
    return {
        "x_in": np.ascontiguousarray(x_local, dtype=np.float32),
        "cw": cw,
        "biasT": P["bias_m"] if mirrored else P["bias_n"],
        "pm": P["pm_m"] if mirrored else P["pm_n"],
    }


def _shared_inputs(layers, P):
    L = len(layers)
    sel = lambda a: np.ascontiguousarray(a[layers[0]:layers[-1] + 1])
    lnp = np.empty((L, 4, 128, D), np.float32)
    for i, l in enumerate(layers):
        for j, nm in enumerate(("ln1_g", "ln1_b", "ln2_g", "ln2_b")):
            lnp[i, j] = np.broadcast_to(P[nm][l][None, :], (128, D))
    sinkb = np.ascontiguousarray(np.broadcast_to(P["sink"][layers[0]:layers[-1] + 1][:, None, :], (L, 128, NH)))
    pscale = np.ascontiguousarray(
        np.transpose(P["pool_scale"][layers[0]:layers[-1] + 1].reshape(L, 8, 128), (0, 2, 1)))
    return {
        "w_in": sel(P["w_in"]), "w_out": sel(P["w_out"]), "w_up": sel(P["w_up"]), "w_down": sel(P["w_down"]),
        "w_pool": sel(P["w_pool"]), "lnp": lnp, "sinkb": sinkb.astype(np.float32),
        "pscale": pscale.astype(np.float32), "ident": np.eye(128, dtype=np.float32),
    }


def _prep(P):
    bucket = _bucket_table()
    rb = np.asarray(P["rel_bias"], np.float32)
    P["bias_n"] = _bias_table(rb, False, bucket)
    P["bias_m"] = _bias_table(rb, True, bucket)
    P["pm_n"] = _pool_mats(False)
    P["pm_m"] = _pool_mats(True)


_PROG_CACHE = {}


def _get_prog(n_layers, T_in, n_out):
    key = (n_layers, T_in, n_out)
    if key not in _PROG_CACHE:
        _PROG_CACHE[key] = build_program(n_layers, T_in, n_out)
    return _PROG_CACHE[key]


N_LAYERS_PER_LAUNCH = 4


def kernel(**inputs):
    P = {k: np.asarray(v, dtype=np.float32) for k, v in inputs.items()}
    _prep(P)
    x = P["x"]
    B = x.shape[0]
    own = SEQ // 2
    cur = x
    nl = N_LAYERS_PER_LAUNCH
    for l0 in range(0, DEPTH, nl):
        layers = list(range(l0, l0 + nl))
        T_in = 16 + 1 + nl if nl < DEPTH else 21
        T_in = 17 + nl
        ntok = T_in * 128
        nc = _get_prog(nl, T_in, own)
        shared = _shared_inputs(layers, P)
        in_maps = []
        for c in range(8):
            b, h = divmod(c, 2)
            if h == 0:
                xl = cur[b, 0:ntok]
            else:
                xl = cur[b, ::-1][0:ntok]
            m = dict(shared)
            m.update(_core_inputs(xl, layers, h == 1, P))
            in_maps.append(m)
        res = run_bass_kernel_spmd(nc, in_maps, core_ids=list(range(8)))
        nxt = np.empty_like(x)
        for c in range(8):
            b, h = divmod(c, 2)
            o = np.asarray(res.results[c]["out"])
            if h == 0:
                nxt[b, 0:own] = o
            else:
                nxt[b, own:] = o[::-1]
        cur = nxt
    return cur
```

```python
import math
from contextlib import ExitStack

import numpy as np
import concourse.bass as bass
import concourse.mybir as mybir
from concourse.bass_utils import run_bass_kernel_spmd

F32 = mybir.dt.float32
BF16 = mybir.dt.bfloat16
AF = mybir.ActivationFunctionType
ALU = mybir.AluOpType

D = 2048
DEPTH = 4
SEQ = 4096
NH = 16
DFF = 5504
NJ = DFF // 128
ALPHA = (2 * DEPTH) ** 0.25
EPS = 1e-5
NEG = -30000.0
POOL_SIZES = (2, 4, 8, 16)


class Op:
    __slots__ = ("eng", "fn", "deps", "is_dma", "inc", "token", "dma_n")

    def __init__(self, eng, fn, is_dma):
        self.eng = eng
        self.fn = fn
        self.deps = []
        self.is_dma = is_dma
        self.inc = False
        self.token = None
        self.dma_n = None


class _Rec:
    def __init__(self):
        self.call = None

    def __getattr__(self, name):
        def f(*args, **kw):
            self.call = (name, args, kw)
            return None
        return f


class Sched:
    ENGS = ("pe", "act", "dve", "pool", "sp")
    ND = 12

    def __init__(self, nc):
        self.nc = nc
        self.ops = {e: [] for e in self.ENGS}
        self.keys = {}
        self.ndma = {e: 0 for e in self.ENGS}

    def _track(self, op, reads, writes):
        for k in reads:
            st = self.keys.setdefault(k, [{}, []])
            for e, w in st[0].items():
                op.deps.append(w)
            st[1].append(op)
        for k in writes:
            st = self.keys.setdefault(k, [{}, []])
            for e, w in st[0].items():
                if w.is_dma or op.is_dma or e != op.eng:
                    op.deps.append(w)
            for r in st[1]:
                if r is op:
                    continue
                if r.is_dma or op.is_dma or r.eng != op.eng:
                    op.deps.append(r)
            st[0] = {(("dma%d" % id(op)) if op.is_dma else op.eng): op}
            st[1] = []

    def op(self, eng, fn, reads=(), writes=()):
        rec = _Rec()
        fn(rec)
        assert rec.call is not None
        o = Op(eng, rec.call, False)
        self._track(o, reads, writes)
        self.ops[eng].append(o)
        return o

    def dma(self, eng, out, in_, reads=(), writes=()):
        o = Op(eng, ("dma_start", (), dict(out=out, in_=in_)), True)
        o.dma_n = self.ndma[eng]
        self.ndma[eng] += 1
        self._track(o, reads, writes)
        self.ops[eng].append(o)
        return o

    def barrier(self):
        lasts = []
        for e in self.ENGS:
            seen_compute = False
            nd = 0
            for o in reversed(self.ops[e]):
                if o.is_dma:
                    if nd < self.ND:
                        lasts.append(o)
                    nd += 1
                elif not seen_compute and o.fn is not None:
                    lasts.append(o)
                    seen_compute = True
                if seen_compute and nd >= self.ND:
                    break
        for e in self.ENGS:
            b = Op(e, None, False)
            b.deps = list(lasts)
            self.ops[e].append(b)
        self.keys = {}

    def emit(self, stack):
        nc = self.nc
        sem = {e: stack.enter_context(nc.semaphore("s_" + e)) for e in self.ENGS}
        dsem = {e: [stack.enter_context(nc.semaphore("d_%s%d" % (e, i))) for i in range(self.ND)]
                for e in self.ENGS if self.ndma[e] > 0}
        for e in self.ENGS:
            for o in self.ops[e]:
                for d in o.deps:
                    d.inc = True
        for e in self.ENGS:
            c = 0
            for o in self.ops[e]:
                if o.is_dma:
                    o.token = (dsem[e][o.dma_n % self.ND], 16 * (o.dma_n // self.ND + 1))
                elif o.inc and o.fn is not None:
                    c += 1
                    o.token = (sem[e], c)
        engmap = {"pe": "tensor", "act": "scalar", "dve": "vector", "pool": "gpsimd", "sp": "sync"}
        block = stack.enter_context(nc.Block())
        for e in self.ENGS:
            ops = self.ops[e]
            if not ops:
                continue

            def body(eng, ops=ops, e=e):
                seen = {}
                for o in ops:
                    need = {}
                    for d in o.deps:
                        if d.token is None:
                            continue
                        s, v = d.token
                        if seen.get(id(s), 0) >= v:
                            continue
                        if need.get(id(s), (None, 0))[1] < v:
                            need[id(s)] = (s, v)
                    if o.is_dma and o.dma_n >= self.ND:
                        s = dsem[e][o.dma_n % self.ND]
                        v = 16 * (o.dma_n // self.ND)
                        if seen.get(id(s), 0) < v and need.get(id(s), (None, 0))[1] < v:
                            need[id(s)] = (s, v)
                    for k, (s, v) in need.items():
                        eng.wait_ge(s, v)
                        seen[k] = v
                    if o.fn is None:
                        continue
                    name, args, kw = o.fn
                    ins = getattr(eng, name)(*args, **kw)
                    if o.is_dma:
                        ins.then_inc(o.token[0], 16)
                    elif o.inc:
                        ins.then_inc(o.token[0], 1)

            getattr(block, engmap[e])(body)


class Arena:
    def __init__(self, t, nwords):
        self.t = t
        self.n = nwords
        self.off = 0

    def _take(self, words):
        a = self.off
        self.off += words
        assert self.off <= self.n, "arena overflow %d > %d" % (self.off, self.n)
        return self.t[:, a:a + words]

    def f32(self, *shape):
        w = int(np.prod(shape))
        ap = self._take(w)
        if len(shape) == 2:
            ap = ap.rearrange("p (a b) -> p a b", a=shape[0])
        elif len(shape) == 3:
            ap = ap.rearrange("p (a b c) -> p a b c", a=shape[0], b=shape[1])
        return ap

    def bf16(self, *shape):
        n = int(np.prod(shape))
        w = (n + 1) // 2
        ap = self._take(w).bitcast(BF16)[:, 0:n]
        if len(shape) == 2:
            ap = ap.rearrange("p (a b) -> p a b", a=shape[0])
        elif len(shape) == 3:
            ap = ap.rearrange("p (a b c) -> p a b c", a=shape[0], b=shape[1])
        return ap


QC_HEADS = []
for _n in (0, 1, 4, 5):
    QC_HEADS.append((2 * _n, 2 * _n + 4))
    QC_HEADS.append((2 * _n + 1, 2 * _n + 5))
HEAD_LOC = {}
for _qc, (_a, _b) in enumerate(QC_HEADS):
    HEAD_LOC[_a] = (_qc, 0)
    HEAD_LOC[_b] = (_qc, 1)

ARENA_WORDS = 48000
NSTG = 4
KVP_BG_EVERY = 10 ** 9
NSG = 3


def build_program(n_layers, T_in, n_out, layer0=0, debug=False):
    nc = bass.Bass("TRN2", target_bir_lowering=False)
    NTOK = T_in * 128
    dt_in = lambda name, shape: nc.dram_tensor(name, shape, F32, kind="ExternalInput").ap()
    x_in = dt_in("x_in", [NTOK, D])
    w_in = dt_in("w_in", [n_layers, D, 2560])
    w_out = dt_in("w_out", [n_layers, D, D])
    w_up = dt_in("w_up", [n_layers, D, 2 * DFF])
    w_down = dt_in("w_down", [n_layers, DFF, D])
    w_pool = dt_in("w_pool", [n_layers, 4, 256, 256])
    lnp = dt_in("lnp", [n_layers, 4, 128, D])
    sinkb = dt_in("sinkb", [n_layers, 128, NH])
    pscale = dt_in("pscale", [n_layers, 128, 8])
    cwin = dt_in("cw", [n_layers, 128, 2 * NJ, 4])
    biasin = dt_in("biasT", [128, NH * 384])
    pmin = dt_in("pm", [128, 16 * 128])
    identin = dt_in("ident", [128, 128])
    out = nc.dram_tensor("out", [n_out, D], F32, kind="ExternalOutput").ap()
    X1 = nc.dram_tensor("X1", [NTOK, D], F32, kind="ExternalOutput" if debug else "Internal").ap()
    X2 = nc.dram_tensor("X2", [NTOK, D], F32, kind="Internal").ap()
    WC_TOTAL = 16 * (2560 + 2048 + 2 * DFF) + NJ * 2048
    wc = nc.dram_tensor("wc", [2, 128, WC_TOTAL], BF16, kind="Internal").ap()

    with ExitStack() as st:
        arena_t = st.enter_context(nc.sbuf_tensor("arena", [128, ARENA_WORDS], F32))
        ps = st.enter_context(nc.psum_tensor("ps", [128, 8, 512], F32))
        S = Sched(nc)
        A = Arena(arena_t, ARENA_WORDS)

        ident = A.f32(128)
        mn_bf = A.bf16(12, 128)
        m0hi = A.bf16(4, 128)
        m0lo = A.bf16(4, 128)
        mhalf = A.f32(1)
        epsc = A.f32(1)
        base_mark = A.off

        S.dma("sp", ident, identin, writes=["ident"])
        S.op("dve", lambda e: e.memset(mhalf, -0.5), writes=["mhalf"])
        S.op("dve", lambda e: e.memset(epsc, EPS), writes=["epsc"])
        tmp_pm = A.f32(16, 128)
        tmp_d = A.f32(4, 128)
        S.dma("sp", tmp_pm, pmin.rearrange("p (a b) -> p a b", a=16), writes=["tmp_pm"])
        S.op("dve", lambda e: e.tensor_copy(out=mn_bf, in_=tmp_pm[:, 0:12, :]), reads=["tmp_pm"], writes=["mn"])
        S.op("dve", lambda e: e.tensor_copy(out=m0hi, in_=tmp_pm[:, 12:16, :]), reads=["tmp_pm"], writes=["m0hi"])
        S.op("dve", lambda e: e.tensor_tensor(out=tmp_d, in0=tmp_pm[:, 12:16, :], in1=m0hi, op=ALU.subtract),
             reads=["tmp_pm", "m0hi"], writes=["tmp_d"])
        S.op("dve", lambda e: e.tensor_copy(out=m0lo, in_=tmp_d), reads=["tmp_d"], writes=["m0lo"])
        S.barrier()
        A.off = base_mark

        class Ctx:
            pass

        def dump(name, ap, reads):
            if not debug:
                return
            shp = list(ap.shape)
            dt = nc.dram_tensor("dbg_" + name, shp, ap.dtype, kind="ExternalOutput").ap()
            S.dma("sp", dt, ap, reads=reads, writes=[("dbg", name)])

        C = Ctx()
        C.stg_i = 0
        C.w4_i = 0
        C.ev_i = 0

        CAST_PAT = ("act", "dve", "act", "dve", "pool")
        C.cast_i = 0

        def cast(out_ap, in_ap, reads, writes):
            eng = CAST_PAT[C.cast_i % len(CAST_PAT)]
            C.cast_i += 1
            if eng == "act":
                S.op("act", lambda e: e.activation(out=out_ap, in_=in_ap, func=AF.Copy), reads=reads, writes=writes)
            else:
                S.op(eng, lambda e: e.tensor_copy(out=out_ap, in_=in_ap), reads=reads, writes=writes)

        C.wc_off = {}
        C.wc_next = 0

        C.par = 0

        def wc_view(piece, shape, par=None):
            if par is None:
                par = C.par
            n = int(np.prod(shape))
            if piece not in C.wc_off:
                C.wc_off[piece] = (C.wc_next, n)
                C.wc_next += n
                assert C.wc_next <= WC_TOTAL
            off, n0 = C.wc_off[piece]
            assert n0 == n
            v = wc[par][:, off:off + n]
            if len(shape) == 2:
                v = v.rearrange("p (a b) -> p a b", a=shape[0])
            return v

        def wc_store(piece, shape, src_bf, src_keys):
            S.dma("pool", wc_view(piece, shape), src_bf, reads=src_keys, writes=[("wc", piece)])

        def wc_load(piece, shape, dst_bf, dst_keys):
            S.dma("sp", dst_bf, wc_view(piece, shape), reads=[("wc", piece)], writes=dst_keys)

        def load_w(dst_bf, dst_key, src_ap, shape, stg, piece=None, first=True):
            if piece is not None and not first:
                wc_load(piece, shape, dst_bf, [dst_key])
                return
            i = stg_next()
            sview = C.stg[i]
            sv = sview[:, 0:shape[0] * shape[1]].rearrange("p (a b) -> p a b", a=shape[0])
            S.dma("sp", sv, src_ap, writes=[("stg", i)])
            cast(dst_bf, sv, [("stg", i)], [dst_key])
            if piece is not None:
                wc_store(piece, shape, dst_bf, [dst_key])

        def evac_copy(out_ap, in_ap, reads, writes, scale=None):
            C.ev_i += 1
            if scale is not None:
                S.op("act", lambda e: e.activation(out=out_ap, in_=in_ap, func=AF.Copy, scale=scale),
                     reads=reads, writes=writes)
            elif C.ev_i % 2:
                S.op("act", lambda e: e.activation(out=out_ap, in_=in_ap, func=AF.Copy), reads=reads, writes=writes)
            else:
                S.op("dve", lambda e: e.tensor_copy(out=out_ap, in_=in_ap), reads=reads, writes=writes)

        def transpose_rows(src_sb, src_key, n, dst3, dst_key_fn, col0, tbank0):
            for c4 in range(4):
                bank = tbank0 + (c4 % 2)
                for c in range(4):
                    cc = c4 * 4 + c
                    S.op("pe", lambda e, cc=cc, c=c, bank=bank: e.transpose(
                        ps[:, bank, c * 128:c * 128 + n], src_sb[0:n, cc * 128:(cc + 1) * 128], ident[0:n, 0:n]),
                        reads=[src_key, "ident"], writes=[("ps", bank)])
                evac_copy(dst3[:, c4 * 4:c4 * 4 + 4, col0:col0 + n],
                          ps[:, bank, :].rearrange("p (a b) -> p a b", a=4)[:, :, 0:n],
                          reads=[("ps", bank)], writes=[dst_key_fn(c4)])

        def layer_norm_multi(items, g_t, b_t, small):
            stt, mv, sc = small
            for (xr, key, n, dst_ap, dst_key, slot) in items:
                for q in range(4):
                    S.op("dve", lambda e: e.bn_stats(out=stt[0:n, slot, q, :], in_=xr[0:n, q * 512:(q + 1) * 512]),
                         reads=[key], writes=[("stt", slot)])
                S.op("dve", lambda e: e.bn_aggr(out=mv[0:n, slot, :], in_=stt[0:n, slot, :, :]),
                     reads=[("stt", slot)], writes=[("mv", slot)])
                S.op("dve", lambda e: e.tensor_scalar(out=sc[0:n, slot, 0:1], in0=mv[0:n, slot, 1:2], scalar1=EPS,
                                                      scalar2=None, op0=ALU.add),
                     reads=[("mv", slot)], writes=[("sc0", slot)])
            for (xr, key, n, dst_ap, dst_key, slot) in items:
                S.op("pool", lambda e: e.tensor_tensor(out=sc[0:n, slot, 1:2], in0=sc[0:n, slot, 0:1], in1=mhalf[0:n, :],
                                                       op=ALU.pow), reads=[("sc0", slot), "mhalf"], writes=[("sc1", slot)])
            for (xr, key, n, dst_ap, dst_key, slot) in items:
                S.op("dve", lambda e: e.tensor_scalar(out=sc[0:n, slot, 2:3], in0=mv[0:n, slot, 0:1], scalar1=-1.0,
                                                      scalar2=sc[0:n, slot, 1:2], op0=ALU.mult, op1=ALU.mult),
                     reads=[("mv", slot), ("sc1", slot)], writes=[("sc2", slot)])
            for (xr, key, n, dst_ap, dst_key, slot) in items:
                S.op("act", lambda e: e.activation(out=xr[0:n, :], in_=xr[0:n, :], func=AF.Identity,
                                                   bias=sc[0:n, slot, 2:3], scale=sc[0:n, slot, 1:2]),
                     reads=[key, ("sc1", slot), ("sc2", slot)], writes=[key])
            for (xr, key, n, dst_ap, dst_key, slot) in items:
                S.op("dve", lambda e: e.tensor_tensor(out=xr[0:n, :], in0=xr[0:n, :], in1=g_t[0:n, :], op=ALU.mult),
                     reads=[key, "lng"], writes=[key])
            for (xr, key, n, dst_ap, dst_key, slot) in items:
                S.op("pool", lambda e: e.tensor_tensor(out=xr[0:n, :], in0=xr[0:n, :], in1=b_t[0:n, :], op=ALU.add),
                     reads=[key, "lnb"], writes=[key])
                S.dma("pool", dst_ap, xr[0:n, :], reads=[key], writes=[dst_key])

        def run_pipeline(steps, PF):
            n = len(steps)
            state = [None] * n
            for i in range(n + PF):
                if i < n:
                    state[i] = steps[i][0]()
                if i >= PF:
                    steps[i - PF][1](state[i - PF])

        def stg_next():
            i = C.stg_i % len(C.stg)
            C.stg_i += 1
            return i

        def load_x_T(src_rows_ap, n, src_keys, dst3, dst_key_fn, col0):
            i = stg_next()
            S.dma("sp", C.stg[i][0:n, :], src_rows_ap, reads=src_keys, writes=[("stg", i)])
            transpose_rows(C.stg[i], ("stg", i), n, dst3, dst_key_fn, col0, 4)

        for L in range(n_layers):
            T_KV = T_in - L
            T_A = T_KV - 1
            last = (L == n_layers - 1)
            xsrc = x_in if L == 0 else X2
            xkey = (lambda t: ("X2", t)) if L > 0 else (lambda t: ("xin", t))
            win_v = w_in[L].rearrange("(k p) c -> p k c", p=128)
            C.par = L % 2
            bg_jobs = []
            if L + 1 < n_layers:
                Ln = L + 1
                wn = w_in[Ln].rearrange("(k p) c -> p k c", p=128)
                for c in range(2):
                    bg_jobs.append((("k", c), (16, 128), [(None, wn[:, :, 1024 + c * 128:1024 + (c + 1) * 128])]))
                for qc, (ha, hb) in enumerate(QC_HEADS):
                    bg_jobs.append((("q", qc), (16, 128), [((0, 64), wn[:, :, ha * 64:(ha + 1) * 64]),
                                                           ((64, 128), wn[:, :, hb * 64:(hb + 1) * 64])]))
                for part in range(5):
                    col0 = 1280 if part == 0 else 1536 + (part - 1) * 256
                    for hh in range(2):
                        bg_jobs.append((("vp", part, hh), (16, 128),
                                        [(None, wn[:, :, col0 + hh * 128:col0 + (hh + 1) * 128])]))
                for cb in range(4):
                    for k4 in range(4):
                        bg_jobs.append((("o", cb, k4), (4, 512), [(None, w_out[Ln][
                            k4 * 512:(k4 + 1) * 512, cb * 512:(cb + 1) * 512].rearrange("(k p) c -> p k c", p=128))]))
                for vg in range(2):
                    for c4 in range(11):
                        npc = min(4, NJ - c4 * 4)
                        for k4 in range(4):
                            src = w_up[Ln][k4 * 512:(k4 + 1) * 512,
                                           vg * DFF + c4 * 512:vg * DFF + c4 * 512 + npc * 128].rearrange(
                                "(k p) c -> p k c", p=128)
                            bg_jobs.append((("ublk", vg, c4, k4, npc), (4, npc * 128), [(None, src)]))
                for cb in range(4):
                    for h4 in range(11):
                        nk = min(4, NJ - h4 * 4)
                        bg_jobs.append((("d", cb, h4), (nk, 512), [(None, w_down[Ln][
                            h4 * 512:h4 * 512 + nk * 128, cb * 512:(cb + 1) * 512].rearrange("(k p) c -> p k c", p=128))]))
            bg_state = {"pending": None, "t": 0, "tick": 0, "stg": None, "bf": None, "every": 2}

            def bg_finish():
                if bg_state["pending"] is not None:
                    piece, shape, slot = bg_state["pending"]
                    n = int(np.prod(shape))
                    sv = bg_state["stg"][slot][:, 0:n].rearrange("p (a b) -> p a b", a=shape[0])
                    bv = bg_state["bf"][slot][:, 0:n].rearrange("p (a b) -> p a b", a=shape[0])
                    if piece[0] == "ublk":
                        _, vg, c4, k4, npc = piece
                        bflat = bg_state["bf"][slot]
                        for jj in range(npc):
                            bvj = bflat[:, jj * 512:(jj + 1) * 512].rearrange("p (a b) -> p a b", a=4)
                            cast(bvj, sv[:, :, jj * 128:(jj + 1) * 128], [("sbg", slot)], [("bbg", slot, jj)])
                            dstv = wc_view(("u", vg, c4 * 4 + jj), (16, 128), (L + 1) % 2)[:, k4 * 4:(k4 + 1) * 4, :]
                            S.dma("pool", dstv, bvj, reads=[("bbg", slot, jj)], writes=[("wcn", piece, jj)])
                    else:
                        cast(bv, sv, [("sbg", slot)], [("bbg", slot, q) for q in range(4)])
                        S.dma("pool", wc_view(piece, shape, (L + 1) % 2), bv, reads=[("bbg", slot, q) for q in range(4)],
                              writes=[("wcn", piece)])
                    bg_state["pending"] = None

            def bg_tick(force=False):
                bg_state["tick"] += 1
                if not force and bg_state["tick"] % bg_state["every"]:
                    return
                bg_finish()
                if bg_jobs:
                    piece, shape, loads = bg_jobs.pop(0)
                    slot = bg_state["t"] % len(bg_state["stg"])
                    bg_state["t"] += 1
                    n = int(np.prod(shape))
                    sv = bg_state["stg"][slot][:, 0:n].rearrange("p (a b) -> p a b", a=shape[0])
                    for sub, src in loads:
                        dstv = sv if sub is None else sv[:, :, sub[0]:sub[1]]
                        S.dma("sp", dstv, src, writes=[("sbg", slot)])
                    bg_state["pending"] = (piece, shape, slot)

            nsg = NSG
            bnds = [(T_A * i) // nsg for i in range(nsg + 1)]
            sgroups = [(bnds[i], bnds[i + 1]) for i in range(nsg)]
            for (ta, tb) in sgroups:
                if tb <= ta:
                    continue
                kv0 = max(ta - 1, 0)
                kv1 = tb + 1
                nres = kv1 - kv0
                A.off = base_mark
                KT = A.bf16(2, nres * 128)
                Vx = A.bf16(nres, 4, 65)
                Pres = A.bf16(nres, 1024)
                nq = tb - ta
                QTres = A.bf16(8, nq * 128)
                res_mark = A.off
                bufA = A.bf16(16, 512)
                wbig = [A.bf16(16, 256) for _ in range(2)]
                w4 = [A.bf16(2048) for _ in range(4)]
                C.stg = [A.f32(2048) for _ in range(NSTG)]
                stg = C.stg
                bg_state["stg"] = [A.f32(2048) for _ in range(2)]
                bg_state["bf"] = [A.bf16(2048) for _ in range(2)]
                bg_state["every"] = KVP_BG_EVERY
                bg_state["t"] = 0
                S.op("dve", lambda e: e.memset(Vx[:, :, :, 64:65], 1.0), writes=["Vones"])
                tiles = list(range(kv0, kv1))
                for g0 in range(0, len(tiles), 4):
                    grp = tiles[g0:g0 + 4]
                    ng = len(grp)
                    ntok = ng * 128
                    kvp_first = (L == 0 and ta == 0 and g0 == 0)
                    for ti, t in enumerate(grp):
                        load_x_T(xsrc[t * 128:(t + 1) * 128, :], 128, [xkey(t)], bufA, lambda c4: ("bufA", c4), ti * 128)
                    bufA_keys = [("bufA", c4) for c4 in range(4)]
                    steps = []

                    def k_prep(c):
                        wi = C.w4_i % 4
                        C.w4_i += 1
                        wk = w4[wi].rearrange("p (a b) -> p a b", a=16)
                        load_w(wk, ("w4", wi), win_v[:, :, 1024 + c * 128:1024 + (c + 1) * 128], (16, 128), stg,
                               piece=("k", c), first=kvp_first)
                        return (wi, wk)

                    def k_run(stt_, c, grp=grp, ntok=ntok):
                        wi, wk = stt_
                        bg_tick()
                        bank = c % 2
                        for kc in range(16):
                            S.op("pe", lambda e: e.matmul(
                                ps[:, bank, 0:ntok], lhsT=wk[:, kc, :], rhs=bufA[:, kc, 0:ntok],
                                start=(kc == 0), stop=(kc == 15)),
                                reads=[("w4", wi)] + bufA_keys, writes=[("ps", bank)])
                        s0 = (grp[0] - kv0) * 128
                        evac_copy(KT[:, c, s0:s0 + ntok], ps[:, bank, 0:ntok], reads=[("ps", bank)],
                                  writes=[("KT", c, t - kv0) for t in grp])

                    def p_prep(part):
                        col0 = 1280 if part == 0 else 1536 + (part - 1) * 256
                        wb_i = part % 2
                        wb = wbig[wb_i]
                        for hh in range(2):
                            load_w(wb[:, :, hh * 128:(hh + 1) * 128], ("wbig", wb_i, hh),
                                   win_v[:, :, col0 + hh * 128:col0 + (hh + 1) * 128], (16, 128), stg,
                                   piece=("vp", part, hh), first=kvp_first)
                        return (wb_i, wb)

                    def p_run(stt_, part, grp=grp):
                        wb_i, wb = stt_
                        bg_tick()
                        for ti, t in enumerate(grp):
                            bank = ti
                            for kc in range(16):
                                S.op("pe", lambda e: e.matmul(
                                    ps[:, bank, 0:256], lhsT=bufA[:, kc, ti * 128:(ti + 1) * 128], rhs=wb[:, kc, :],
                                    start=(kc == 0), stop=(kc == 15)),
                                    reads=[("wbig", wb_i, 0), ("wbig", wb_i, 1)] + bufA_keys, writes=[("ps", bank)])
                            sl = t - kv0
                            if part == 0:
                                evac_copy(Vx[:, sl, :, 0:64], ps[:, bank, 0:256].rearrange("p (a b) -> p a b", a=4),
                                          reads=[("ps", bank)], writes=[("V", sl)])
                            else:
                                evac_copy(Pres[:, sl, (part - 1) * 256:part * 256], ps[:, bank, 0:256],
                                          reads=[("ps", bank)], writes=[("P", sl, part - 1)])

                    qlo = max(grp[0], ta)
                    qhi = min(grp[-1] + 1, tb)
                    def q_prep(pi, n):
                        if not kvp_first:
                            outs = []
                            for ab in range(2):
                                wi = C.w4_i % 4
                                C.w4_i += 1
                                wq = w4[wi].rearrange("p (a b) -> p a b", a=16)
                                wc_load(("q", pi * 2 + ab), (16, 128), wq, [("w4", wi)])
                                outs.append((wi, wq))
                            return outs
                        ia = stg_next()
                        ib = stg_next()
                        sa = stg[ia].rearrange("p (a b) -> p a b", a=16)
                        sbb = stg[ib].rearrange("p (a b) -> p a b", a=16)
                        S.dma("sp", sa, win_v[:, :, n * 128:(n + 1) * 128], writes=[("stg", ia)])
                        S.dma("sp", sbb, win_v[:, :, (n + 2) * 128:(n + 3) * 128], writes=[("stg", ib)])
                        outs = []
                        for ab in range(2):
                            wi = C.w4_i % 4
                            C.w4_i += 1
                            wq = w4[wi].rearrange("p (a b) -> p a b", a=16)
                            C.cast_i = 0 if ab == 0 else 1
                            cast(wq[:, :, 0:64], sa[:, :, ab * 64:(ab + 1) * 64], [("stg", ia)], [("w4", wi)])
                            C.cast_i = 0 if ab == 0 else 1
                            cast(wq[:, :, 64:128], sbb[:, :, ab * 64:(ab + 1) * 64], [("stg", ib)], [("w4", wi)])
                            wc_store(("q", pi * 2 + ab), (16, 128), wq, [("w4", wi)])
                            outs.append((wi, wq))
                        return outs

                    def q_run(outs, pi, qlo=qlo, qhi=qhi, grp=grp):
                        bg_tick()
                        c_lo = (qlo - grp[0]) * 128
                        c_hi = (qhi - grp[0]) * 128
                        nn = c_hi - c_lo
                        for ab in range(2):
                            wi, wq = outs[ab]
                            qc = pi * 2 + ab
                            bank = qc % 4
                            for kc in range(16):
                                S.op("pe", lambda e: e.matmul(
                                    ps[:, bank, 0:nn], lhsT=wq[:, kc, :], rhs=bufA[:, kc, c_lo:c_hi],
                                    start=(kc == 0), stop=(kc == 15)),
                                    reads=[("w4", wi)] + bufA_keys, writes=[("ps", bank)])
                            evac_copy(QTres[:, qc, (qlo - ta) * 128:(qhi - ta) * 128], ps[:, bank, 0:nn], reads=[("ps", bank)],
                                      writes=[("QT", qc, t - ta) for t in range(qlo, qhi)], scale=0.125)


                    for c in range(2):
                        steps.append((lambda c=c: k_prep(c), lambda s_, c=c: k_run(s_, c)))
                    if qhi > qlo:
                        for pi, n in enumerate((0, 1, 4, 5)):
                            steps.append((lambda pi=pi, n=n: q_prep(pi, n), lambda o_, pi=pi: q_run(o_, pi)))
                    for part in range(5):
                        steps.append((lambda part=part: p_prep(part), lambda s_, part=part: p_run(s_, part)))
                    run_pipeline(steps, 1)
                bg_finish()
                S.barrier()

                A.off = res_mark
                biasT = A.bf16(NH, 384)
                bufA = A.bf16(16, 512)
                w4 = [A.bf16(2048) for _ in range(4)]
                pT = [A.bf16(384) for _ in range(3)]
                DT = A.bf16(8, 128)
                wpool_sb = A.bf16(4, 2, 256)
                C.stg = [A.f32(2048) for _ in range(NSTG)]
                stg = C.stg
                xr = [A.f32(2048) for _ in range(4)]
                g_t = A.f32(2048)
                b_t = A.f32(2048)
                sb = [A.f32(384) for _ in range(2)]
                attn = A.f32(1024)
                esink = A.f32(NH)
                den = A.f32(NH)
                rec = A.f32(NH)
                psc = A.f32(8)
                stt = A.f32(4, 4, 6)
                mv = A.f32(4, 2)
                sc = A.f32(4, 4)
                small = (stt, mv, sc)
                S.dma("sp", g_t, lnp[L, 0], writes=["lng"])
                S.dma("sp", b_t, lnp[L, 1], writes=["lnb"])
                S.dma("sp", esink, sinkb[L], writes=["esink"])
                S.op("act", lambda e: e.activation(out=esink, in_=esink, func=AF.Exp), reads=["esink"], writes=["esink"])
                S.dma("sp", psc, pscale[L], writes=["psc"])
                bflat = biasT.rearrange("p a b -> p (a b)")
                for i3 in range(3):
                    i = stg_next()
                    S.dma("sp", stg[i], biasin[:, i3 * 2048:(i3 + 1) * 2048], writes=[("stg", i)])
                    C.cast_i = i3
                    cast(bflat[:, i3 * 2048:(i3 + 1) * 2048], stg[i], [("stg", i)], [("biasT", i3)])
                i = stg_next()
                wp_st = stg[i].rearrange("p (g c d) -> p g c d", g=4, c=2)
                S.dma("sp", wp_st, w_pool[L].rearrange("g (c p) d -> p g c d", p=128), writes=[("stg", i)])
                S.op("dve", lambda e: e.tensor_copy(out=wpool_sb, in_=wp_st), reads=[("stg", i)], writes=["wpool"])

                qtiles = list(range(ta, tb))
                for g0 in range(0, len(qtiles), 4):
                    grp = qtiles[g0:g0 + 4]
                    ng = len(grp)
                    ntok = ng * 128
                    bufA_keys = [("bufA", c4) for c4 in range(4)]
                    mix_first = (L == 0 and ta == 0 and g0 == 0)
                    for ti, t in enumerate(grp):
                        sl = t - kv0
                        blocks = [b for b in range(3) if t + b - 1 >= 0]
                        c0 = blocks[0] * 128
                        for r in range(2):
                            for c in range(4):
                                c8 = r * 4 + c
                                g = c8 // 2
                                mats = []
                                if t == 0:
                                    mats.append((sl, m0hi[:, g, :], "m0hi"))
                                    mats.append((sl, m0lo[:, g, :], "m0lo"))
                                    mats.append((sl + 1, mn_bf[:, g * 3 + 2, :], "mn"))
                                else:
                                    for b in range(3):
                                        mats.append((sl + b - 1, mn_bf[:, g * 3 + b, :], "mn"))
                                for mi, (slp, mat, mkey) in enumerate(mats):
                                    S.op("pe", lambda e: e.matmul(
                                        ps[:, 6, c * 128:(c + 1) * 128], lhsT=Pres[:, slp, c8 * 128:(c8 + 1) * 128], rhs=mat,
                                        start=(mi == 0), stop=(mi == len(mats) - 1)),
                                        reads=[("P", slp, c8 // 2), mkey], writes=[("ps", 6)])
                            S.op("dve", lambda e: e.tensor_copy(
                                out=DT[:, r * 4:r * 4 + 4, :], in_=ps[:, 6, :].rearrange("p (a b) -> p a b", a=4)),
                                reads=[("ps", 6)], writes=[("DT", r)])
                        for r in range(2):
                            for c in range(4):
                                dc = r * 4 + c
                                g = dc // 2
                                dd = dc % 2
                                for cc in range(2):
                                    S.op("pe", lambda e: e.matmul(
                                        ps[:, 7, c * 128:(c + 1) * 128], lhsT=wpool_sb[:, g, cc, dd * 128:(dd + 1) * 128],
                                        rhs=DT[:, 2 * g + cc, :], start=(cc == 0), stop=(cc == 1)),
                                        reads=["wpool", ("DT", g // 2)], writes=[("ps", 7)])
                            for c in range(4):
                                dc = r * 4 + c
                                S.op("act", lambda e: e.activation(
                                    out=bufA[:, 8 + dc, ti * 128:(ti + 1) * 128], in_=ps[:, 7, c * 128:(c + 1) * 128],
                                    func=AF.Identity, scale=psc[:, dc:dc + 1]),
                                    reads=[("ps", 7), "psc"], writes=[("bufA", 2 + r)])

                        def s_mm(h):
                            qc, hf = HEAD_LOC[h]
                            kvc = (h // 4) // 2
                            sbank = 4 + h % 2
                            for b in blocks:
                                S.op("pe", lambda e: e.matmul(
                                    ps[:, sbank, b * 128:(b + 1) * 128],
                                    lhsT=KT[hf * 64:(hf + 1) * 64, kvc, (sl + b - 1) * 128:(sl + b) * 128],
                                    rhs=QTres[hf * 64:(hf + 1) * 64, qc, (t - ta) * 128:(t - ta + 1) * 128], start=True, stop=True),
                                    reads=[("KT", kvc, sl + b - 1), ("QT", qc, t - ta)], writes=[("ps", sbank)])

                        def soft(h):
                            sbank = 4 + h % 2
                            sbi = h % 2
                            pti = h % 3
                            S.op("dve", lambda e: e.tensor_tensor(
                                out=sb[sbi][:, c0:384], in0=ps[:, sbank, c0:384], in1=biasT[:, h, c0:384], op=ALU.add),
                                reads=[("ps", sbank)] + [("biasT", i3) for i3 in range(3)], writes=[("sb", sbi)])
                            S.op("act", lambda e: e.activation(
                                out=pT[pti][:, c0:384], in_=sb[sbi][:, c0:384], func=AF.Exp),
                                reads=[("sb", sbi)], writes=[("pT", pti)])

                        def pv(h):
                            kv = h // 4
                            pti = h % 3
                            obank = h // 7
                            oslot = (h % 7) * 65
                            for b in blocks:
                                S.op("pe", lambda e: e.matmul(
                                    ps[:, obank, oslot:oslot + 65], lhsT=pT[pti][:, b * 128:(b + 1) * 128],
                                    rhs=Vx[:, sl + b - 1, kv, :], start=(b == blocks[0]), stop=(b == blocks[-1])),
                                    reads=[("pT", pti), ("V", sl + b - 1), "Vones"], writes=[("ps", obank)])

                        s_mm(0)
                        for h in range(NH):
                            if h + 1 < NH:
                                s_mm(h + 1)
                            soft(h)
                            pv(h)
                        for obank in range(3):
                            h0 = obank * 7
                            nh = min(7, NH - h0)
                            pso = ps[:, obank, 0:nh * 65].rearrange("p (a b) -> p a b", a=nh)
                            S.op("dve", lambda e: e.tensor_tensor(
                                out=den[:, h0:h0 + nh].unsqueeze(2), in0=pso[:, :, 64:65],
                                in1=esink[:, h0:h0 + nh].unsqueeze(2), op=ALU.add),
                                reads=[("ps", obank), "esink"], writes=[("den", obank)])
                            S.op("dve", lambda e: e.reciprocal(out=rec[:, h0:h0 + nh], in_=den[:, h0:h0 + nh]),
                                 reads=[("den", obank)], writes=[("rec", obank)])
                            S.op("dve", lambda e: e.tensor_tensor(
                                out=attn[:, h0 * 64:(h0 + nh) * 64].rearrange("p (a b) -> p a b", a=nh),
                                in0=pso[:, :, 0:64],
                                in1=rec[:, h0:h0 + nh].unsqueeze(2).to_broadcast([128, nh, 64]), op=ALU.mult),
                                reads=[("ps", obank), ("rec", obank)], writes=[("attn", obank)])
                        for r in range(2):
                            for c in range(4):
                                cc = r * 4 + c
                                S.op("pe", lambda e: e.transpose(
                                    ps[:, 3, c * 128:(c + 1) * 128], attn[:, cc * 128:(cc + 1) * 128], ident),
                                    reads=[("attn", 0), ("attn", 1), ("attn", 2), "ident"], writes=[("ps", 3)])
                            evac_copy(bufA[:, r * 4:r * 4 + 4, ti * 128:(ti + 1) * 128],
                                      ps[:, 3, :].rearrange("p (a b) -> p a b", a=4),
                                      reads=[("ps", 3)], writes=[("bufA", r)])
                    if debug and L == 0 and ta == 0 and g0 == 0:
                        dump("attn", attn, [("attn", 0), ("attn", 1), ("attn", 2)])
                        dump("mixT", bufA, [("bufA", q) for q in range(4)])
                    wo_v = w_out[L]
                    for ti, t in enumerate(grp):
                        S.dma("sp", xr[ti], xsrc[t * 128:(t + 1) * 128, :], reads=[xkey(t)], writes=[("xr", ti)])

                    def o_prep(cb, k4):
                        wi = C.w4_i % 4
                        C.w4_i += 1
                        wo = w4[wi].rearrange("p (a b) -> p a b", a=4)
                        load_w(wo, ("w4", wi),
                               wo_v[k4 * 512:(k4 + 1) * 512, cb * 512:(cb + 1) * 512].rearrange("(k p) c -> p k c", p=128),
                               (4, 512), stg, piece=("o", cb, k4), first=mix_first)
                        return (wi, wo)

                    def o_run(s_, cb, k4, ng=ng, grp=grp):
                        wi, wo = s_
                        for ti in range(ng):
                            for kk in range(4):
                                kc = k4 * 4 + kk
                                S.op("pe", lambda e: e.matmul(
                                    ps[:, ti, :], lhsT=bufA[:, kc, ti * 128:(ti + 1) * 128], rhs=wo[:, kk, :],
                                    start=(kc == 0), stop=(kc == 15)),
                                    reads=[("w4", wi)] + bufA_keys, writes=[("ps", ti)])
                        if k4 == 3:
                            for ti in range(ng):
                                S.op("dve", lambda e: e.scalar_tensor_tensor(
                                    out=xr[ti][:, cb * 512:(cb + 1) * 512], in0=xr[ti][:, cb * 512:(cb + 1) * 512],
                                    scalar=ALPHA, in1=ps[:, ti, :], op0=ALU.mult, op1=ALU.add),
                                    reads=[("ps", ti), ("xr", ti)], writes=[("xr", ti)])
                            if cb == 3:
                                layer_norm_multi([(xr[ti], ("xr", ti), 128, X1[t * 128:(t + 1) * 128, :], ("X1", t), ti)
                                                  for ti, t in enumerate(grp)], g_t, b_t, small)

                    steps = [(lambda cb=cb, k4=k4: o_prep(cb, k4), lambda s_, cb=cb, k4=k4: o_run(s_, cb, k4))
                             for cb in range(4) for k4 in range(4)]
                    run_pipeline(steps, 2)
                S.barrier()

            A.off = base_mark
            bufX = A.bf16(16, 512)
            hT = A.bf16(NJ, 512)
            w4 = [A.bf16(2048) for _ in range(4)]
            C.stg = [A.f32(2048) for _ in range(3)]
            stg = C.stg
            stg_bg = [A.f32(2048) for _ in range(2)]
            bf_bg = [A.bf16(2048) for _ in range(2)]
            xr = [A.f32(2048) for _ in range(4)]
            g_t = A.f32(2048)
            b_t = A.f32(2048)
            tvb = [A.f32(512) for _ in range(2)]
            tgb = [A.f32(512) for _ in range(2)]
            cw = A.f32(2 * NJ, 4)
            stt = A.f32(4, 4, 6)
            mv = A.f32(4, 2)
            sc = A.f32(4, 4)
            small = (stt, mv, sc)
            S.dma("sp", g_t, lnp[L, 2], writes=["lng"])
            S.dma("sp", b_t, lnp[L, 3], writes=["lnb"])
            S.dma("sp", cw, cwin[L], writes=["cw"])
            wup_v = w_up[L].rearrange("(k p) c -> p k c", p=128)
            wd_v = w_down[L]
            bg_state["stg"] = stg_bg
            bg_state["bf"] = bf_bg
            bg_state["every"] = 2
            bg_state["t"] = 0
            x1_avail = T_A * 128
            n_ffn = n_out if last else min(T_A * 128, n_out + 129 * (n_layers - 1 - L))
            if n_ffn < T_A * 128 and not last:
                S.op("dve", lambda e: e.memset(xr[0], 0.0), writes=[("xr", 0)])
                z0 = n_ffn
                while z0 < T_A * 128:
                    zn = min(128, T_A * 128 - z0)
                    S.dma("sp", X2[z0:z0 + zn, :], xr[0][0:zn, :], reads=[("xr", 0)], writes=[("X2z", z0)])
                    z0 += zn
            WIN = 510
            nwin = (n_ffn + WIN - 1) // WIN
            WEVEN = (n_ffn + nwin - 1) // nwin
            w0 = 0
            jcount = 0
            while w0 < n_ffn:
                W = min(WEVEN, n_ffn - w0)
                NU = W + 2
                bufX_keys = [("bufX", c4) for c4 in range(4)]
                tok_lo = w0 - 1
                tok_hi = w0 + W + 1
                col = 0
                if tok_lo < 0:
                    S.op("dve", lambda e: e.memset(bufX[:, :, 0:1], 0.0), writes=bufX_keys)
                    tok_lo = 0
                    col = 1
                zero_tail = False
                if tok_hi > x1_avail:
                    tok_hi = x1_avail
                    zero_tail = True
                tk = tok_lo
                while tk < tok_hi:
                    n = min(128, tok_hi - tk)
                    load_x_T(X1[tk:tk + n, :], n, [("X1", tt) for tt in range(tk // 128, (tk + n - 1) // 128 + 1)],
                             bufX, lambda c4: ("bufX", c4), col)
                    tk += n
                    col += n
                if zero_tail:
                    S.op("dve", lambda e: e.memset(bufX[:, :, col:col + 1], 0.0), writes=bufX_keys)
                    col += 1
                assert col == NU, (col, NU)
                nsub = (W + 127) // 128
                subn = [min(128, W - s * 128) for s in range(nsub)]

                def u_prep(j):
                    wis = []
                    for vg in range(2):
                        wi = C.w4_i % 4
                        C.w4_i += 1
                        wv = w4[wi].rearrange("p (a b) -> p a b", a=16)
                        load_w(wv, ("w4", wi), wup_v[:, :, vg * DFF + j * 128:vg * DFF + (j + 1) * 128], (16, 128), stg,
                               piece=("u", vg, j), first=(L == 0 and w0 == 0))
                        wis.append((wi, wv))
                    return wis

                def u_run(wis, j, W=W, NU=NU):
                    if L > 0 or w0 > 0:
                        bg_tick()
                    pb = j % 2
                    banks = (pb, 2 + pb)
                    for vg in range(2):
                        wi, wv = wis[vg]
                        for kc in range(16):
                            S.op("pe", lambda e: e.matmul(
                                ps[:, banks[vg], 0:NU], lhsT=wv[:, kc, :], rhs=bufX[:, kc, 0:NU],
                                start=(kc == 0), stop=(kc == 15)),
                                reads=[("w4", wi)] + bufX_keys, writes=[("ps", banks[vg])])
                    tv = tvb[pb]
                    tg = tgb[pb]
                    for vg, tt, tkey in ((0, tv, ("tv", pb)), (1, tg, ("tg", pb))):
                        jj = vg * NJ + j
                        bank = banks[vg]
                        S.op("act", lambda e: e.activation(
                            out=tt[:, 0:W], in_=ps[:, bank, 0:W], func=AF.Identity,
                            bias=cw[:, jj, 3:4], scale=cw[:, jj, 0:1]),
                            reads=[("ps", bank), "cw"], writes=[tkey])
                        for k in (1, 2):
                            S.op("dve", lambda e: e.scalar_tensor_tensor(
                                out=tt[:, 0:W], in0=ps[:, bank, k:k + W], scalar=cw[:, jj, k:k + 1], in1=tt[:, 0:W],
                                op0=ALU.mult, op1=ALU.add),
                                reads=[("ps", bank), "cw", tkey], writes=[tkey])
                    S.op("act", lambda e: e.activation(out=tg[:, 0:W], in_=tg[:, 0:W], func=AF.Gelu_apprx_tanh),
                         reads=[("tg", pb)], writes=[("tg", pb)])
                    S.op("pool", lambda e: e.tensor_tensor(
                        out=hT[:, j, 0:W], in0=tg[:, 0:W], in1=tv[:, 0:W], op=ALU.mult),
                        reads=[("tg", pb), ("tv", pb)], writes=[("hT", j)])

                steps = [(lambda j=j: u_prep(j), lambda s_, j=j: u_run(s_, j)) for j in range(NJ)]
                run_pipeline(steps, 1)

                for s in range(nsub):
                    r0 = w0 + s * 128
                    S.dma("sp", xr[s][0:subn[s], :], X1[r0:r0 + subn[s], :],
                          reads=[("X1", tt) for tt in range(r0 // 128, (r0 + subn[s] - 1) // 128 + 1)], writes=[("xr", s)])

                def d_prep(cb, h4):
                    nk = min(4, NJ - h4 * 4)
                    wi = C.w4_i % 4
                    C.w4_i += 1
                    wd = w4[wi].rearrange("p (a b) -> p a b", a=4)
                    if L == 0 and w0 == 0:
                        i = stg_next()
                        sv = stg[i].rearrange("p (a b) -> p a b", a=4)
                        S.dma("sp", sv[:, 0:nk, :],
                              wd_v[h4 * 512:h4 * 512 + nk * 128, cb * 512:(cb + 1) * 512].rearrange("(k p) c -> p k c", p=128),
                              writes=[("stg", i)])
                        cast(wd[:, 0:nk, :], sv[:, 0:nk, :], [("stg", i)], [("w4", wi)])
                        wc_store(("d", cb, h4), (nk, 512), wd[:, 0:nk, :], [("w4", wi)])
                    else:
                        wc_load(("d", cb, h4), (nk, 512), wd[:, 0:nk, :], [("w4", wi)])
                    return (wi, wd, nk)

                def d_run(s_, cb, h4, nsub=nsub, subn=subn, w0=w0):
                    wi, wd, nk = s_
                    if L > 0 or w0 > 0:
                        bg_tick()
                    for s in range(nsub):
                        for kk in range(nk):
                            hc = h4 * 4 + kk
                            S.op("pe", lambda e: e.matmul(
                                ps[0:subn[s], 4 + s, :], lhsT=hT[:, hc, s * 128:s * 128 + subn[s]], rhs=wd[:, kk, :],
                                start=(hc == 0), stop=(hc == NJ - 1)),
                                reads=[("w4", wi), ("hT", hc)], writes=[("ps", 4 + s)])
                    if h4 == 10:
                        for s in range(nsub):
                            S.op("dve", lambda e: e.scalar_tensor_tensor(
                                out=xr[s][0:subn[s], cb * 512:(cb + 1) * 512], in0=xr[s][0:subn[s], cb * 512:(cb + 1) * 512],
                                scalar=ALPHA, in1=ps[0:subn[s], 4 + s, :], op0=ALU.mult, op1=ALU.add),
                                reads=[("ps", 4 + s), ("xr", s)], writes=[("xr", s)])
                        if cb == 3:
                            dstT = out if last else X2
                            layer_norm_multi([(xr[s], ("xr", s), subn[s], dstT[w0 + s * 128:w0 + s * 128 + subn[s], :],
                                               ("ow", w0 + s * 128), s) for s in range(nsub)], g_t, b_t, small)

                steps = [(lambda cb=cb, h4=h4: d_prep(cb, h4), lambda s_, cb=cb, h4=h4: d_run(s_, cb, h4))
                         for cb in range(4) for h4 in range(11)]
                run_pipeline(steps, 2)
                w0 += W
            while bg_jobs:
                bg_tick(force=True)
            bg_finish()
            S.barrier()
        S.emit(st)
    return nc


def _bucket_table():
    import jax
    import jax.numpy as jnp
    cpu = jax.devices("cpu")[0]
    with jax.default_device(cpu):
        rel = jnp.arange(-128, 129)
        half = 16
        max_exact = 8
        base = jnp.where(rel > 0, half, 0)
        n = jnp.abs(rel)
        nf = jnp.maximum(n, 1).astype(jnp.float32)
        large = max_exact + (jnp.log(nf / max_exact) / math.log(128 / max_exact) * (half - max_exact)).astype(jnp.int32)
        large = jnp.minimum(large, half - 1)
        b = base + jnp.where(n < max_exact, n, large)
        return np.asarray(b)


def _bias_table(rel_bias, mirrored, bucket):
    s = np.arange(128)[:, None, None]
    b = np.arange(3)[None, :, None]
    q = np.arange(128)[None, None, :]
    relp = (b - 1) * 128 + s - q
    valid = np.abs(relp) <= 128
    relg = -relp if mirrored else relp
    idx = bucket[np.clip(relg, -128, 128) + 128]
    tab = rel_bias[idx]
    tab = np.where(valid[..., None], tab, np.float32(NEG)).astype(np.float32)
    tab = np.transpose(tab, (0, 3, 1, 2))
    return np.ascontiguousarray(tab.reshape(128, NH * 384))


def _pool_mats(mirrored):
    pm = np.zeros((128, 16, 128), np.float32)
    for g, w in enumerate(POOL_SIZES):
        for t in range(128):
            if mirrored:
                lo, hi = t - w // 2 + 1, t + w // 2 + 1
            else:
                lo, hi = t - w // 2, t + w // 2
            for p in range(lo, hi):
                b = (p + 128) // 128
                pm[p - (b - 1) * 128, g * 3 + b, t] += 1.0 / w
            pm[t, g * 3 + 1, t] -= 1.0
            lo0 = max(lo, 0)
            cnt = hi - lo0
            for p in range(lo0, min(hi, 128)):
                pm[p, 12 + g, t] += np.float32(1.0) / np.float32(cnt)
            pm[t, 12 + g, t] -= 1.0
    return np.ascontiguousarray(pm.reshape(128, 16 * 128))


def _core_inputs(x_local, layers, mirrored, P):
    L = len(layers)
    cw = np.empty((L, 128, 2 * NJ, 4), np.float32)
    for i, l in enumerate(layers):
        taps = P["conv_w"][l][::-1] if mirrored else P["conv_w"][l]
        for k in range(3):
            cw[i, :, :, k] = taps[k].reshape(2 * NJ, 128).T
        cw[i, :, :, 3] = P["conv_b"][l].reshape(2 * NJ, 128).T
    return {
        "x_in": np.ascontiguousarray(x_local, dtype=np.float32),
        "cw": cw,
        "biasT": P["bias_m"] if mirrored else P["bias_n"],
        "pm": P["pm_m"] if mirrored else P["pm_n"],
    }


def _shared_inputs(layers, P):
    L = len(layers)
    sel = lambda a: np.ascontiguousarray(a[layers[0]:layers[-1] + 1])
    lnp = np.empty((L, 4, 128, D), np.float32)
    for i, l in enumerate(layers):
        for j, nm in enumerate(("ln1_g", "ln1_b", "ln2_g", "ln2_b")):
            lnp[i, j] = np.broadcast_to(P[nm][l][None, :], (128, D))
    sinkb = np.ascontiguousarray(np.broadcast_to(P["sink"][layers[0]:layers[-1] + 1][:, None, :], (L, 128, NH)))
    pscale = np.ascontiguousarray(
        np.transpose(P["pool_scale"][layers[0]:layers[-1] + 1].reshape(L, 8, 128), (0, 2, 1)))
    return {
        "w_in": sel(P["w_in"]), "w_out": sel(P["w_out"]), "w_up": sel(P["w_up"]), "w_down": sel(P["w_down"]),
        "w_pool": sel(P["w_pool"]), "lnp": lnp, "sinkb": sinkb.astype(np.float32),
        "pscale": pscale.astype(np.float32), "ident": np.eye(128, dtype=np.float32),
    }


def _prep(P):
    bucket = _bucket_table()
    rb = np.asarray(P["rel_bias"], np.float32)
    P["bias_n"] = _bias_table(rb, False, bucket)
    P["bias_m"] = _bias_table(rb, True, bucket)
    P["pm_n"] = _pool_mats(False)
    P["pm_m"] = _pool_mats(True)


_PROG_CACHE = {}


def _get_prog(n_layers, T_in, n_out):
    key = (n_layers, T_in, n_out)
    if key not in _PROG_CACHE:
        _PROG_CACHE[key] = build_program(n_layers, T_in, n_out)
    return _PROG_CACHE[key]


N_LAYERS_PER_LAUNCH = 4


def kernel(**inputs):
    P = {k: np.asarray(v, dtype=np.float32) for k, v in inputs.items()}
    _prep(P)
    x = P["x"]
    B = x.shape[0]
    own = SEQ // 2
    cur = x
    nl = N_LAYERS_PER_LAUNCH
    for l0 in range(0, DEPTH, nl):
        layers = list(range(l0, l0 + nl))
        T_in = 16 + 1 + nl if nl < DEPTH else 21
        T_in = 17 + nl
        ntok = T_in * 128
        nc = _get_prog(nl, T_in, own)
        shared = _shared_inputs(layers, P)
        in_maps = []
        for c in range(8):
            b, h = divmod(c, 2)
            if h == 0:
                xl = cur[b, 0:ntok]
            else:
                xl = cur[b, ::-1][0:ntok]
            m = dict(shared)
            m.update(_core_inputs(xl, layers, h == 1, P))
            in_maps.append(m)
        res = run_bass_kernel_spmd(nc, in_maps, core_ids=list(range(8)))
        nxt = np.empty_like(x)
        for c in range(8):
            b, h = divmod(c, 2)
            o = np.asarray(res.results[c]["out"])
            if h == 0:
                nxt[b, 0:own] = o
            else:
                nxt[b, own:] = o[::-1]
        cur = nxt
    return cur
```
